# Optimizing a Trainium2 kernel written in Bass

```python
import jax, jax.numpy as jnp
from jax import lax
import numpy as np

D_MODEL = 1024
BATCH = 8
SEQ = 4096
DEPTH = 2
DEC_BATCH = 32
DEC_SEQ = 32
PAST_LEN = 2048

CHUNK = 64
N_META = 16
Q_BLOCK = 128
N_MIXERS = 2
N_FOX = (DEPTH + 1) // 2
N_MLSTM = DEPTH // 2
FOX_HEADS = 16
FOX_HEAD_DIM = D_MODEL // FOX_HEADS
MLSTM_HEADS = 4
MLSTM_DV = D_MODEL // MLSTM_HEADS
MLSTM_DK = MLSTM_DV // 2
D_FF = 4 * D_MODEL
EPS = 1e-6
NEG = -1e30
FOX_IN = 3 * D_MODEL + FOX_HEADS
FOX_SPLITS = [D_MODEL, 2 * D_MODEL, 3 * D_MODEL]
_HK = MLSTM_HEADS * MLSTM_DK
_HV = MLSTM_HEADS * MLSTM_DV
MLSTM_IN = 2 * _HK + 2 * _HV + 2 * MLSTM_HEADS
MLSTM_SPLITS = [_HK, 2 * _HK, 2 * _HK + _HV, 2 * _HK + 2 * _HV, 2 * _HK + 2 * _HV + MLSTM_HEADS]
MLSTM_PAD = (-N_META) % CHUNK

kernel_name = "fox_mlstm_streaming_step"


def rmsnorm(x, g):
    xf = x.astype(jnp.float32)
    y = xf * lax.rsqrt(jnp.mean(xf * xf, axis=-1, keepdims=True) + EPS)
    return (y * g.astype(jnp.float32)).astype(x.dtype)


def sq_relu_mlp(h, w_up, w_down):
    u = jnp.einsum('btd,df->btf', h, w_up)
    return jnp.einsum('btf,fd->btd', jnp.square(jax.nn.relu(u)), w_down)


def fox_attend(q, k, v, Fq, Fk, q_offset):
    B, Tq, H, dh = q.shape
    Tk = k.shape[1]
    blk = min(Q_BLOCK, Tq)
    nb = -(-Tq // blk)
    pad = nb * blk - Tq
    qp = jnp.pad(q, ((0, 0), (0, pad), (0, 0), (0, 0)))
    Fqp = jnp.pad(Fq, ((0, 0), (0, pad), (0, 0)))
    q_blocks = qp.reshape(B, nb, blk, H, dh).transpose(1, 0, 2, 3, 4)
    F_blocks = Fqp.reshape(B, nb, blk, H).transpose(1, 0, 3, 2)
    starts = q_offset + jnp.arange(nb) * blk
    Fk_t = Fk.transpose(0, 2, 1)
    kpos = jnp.arange(Tk)
    scale = FOX_HEAD_DIM ** -0.5

    def one_block(args):
        qb, Fb, s0 = args
        qpos = s0 + jnp.arange(blk)
        logits = jnp.einsum('bqhd,bkhd->bhqk', qb, k, preferred_element_type=jnp.float32) * scale
        logits = logits + Fb[..., :, None] - Fk_t[..., None, :]
        logits = jnp.where(kpos[None, None, None, :] <= qpos[None, None, :, None], logits, -jnp.inf)
        p = jax.nn.softmax(logits, axis=-1)
        return jnp.einsum('bhqk,bkhd->bqhd', p.astype(v.dtype), v)

    out = lax.map(one_block, (q_blocks, F_blocks, starts))
    return out.transpose(1, 0, 2, 3, 4).reshape(B, nb * blk, H, dh)[:, :Tq]


def fox_mixer(h, w_in, b_f, g_q, g_k, w_out, past):
    B, T, _ = h.shape
    proj = jnp.einsum('btd,de->bte', h, w_in)
    q, k, v, f = jnp.split(proj, FOX_SPLITS, axis=-1)
    shp = (B, T, FOX_HEADS, FOX_HEAD_DIM)
    q = rmsnorm(q.reshape(shp), g_q)
    k = rmsnorm(k.reshape(shp), g_k)
    v = v.reshape(shp)
    logf = jax.nn.log_sigmoid((f + b_f).astype(jnp.float32))
    if past is None:
        k_all, v_all, logf_all, offset = k, v, logf, 0
    else:
        k_past, v_past, logf_past = past
        offset = k_past.shape[1]
        k_all = jnp.concatenate([k_past.astype(k.dtype), k], axis=1)
        v_all = jnp.concatenate([v_past.astype(v.dtype), v], axis=1)
        logf_all = jnp.concatenate([logf_past.astype(jnp.float32), logf], axis=1)
    F = jnp.cumsum(logf_all, axis=1)
    o = fox_attend(q, k_all, v_all, F[:, offset:], F, offset)
    y = jnp.einsum('bte,ed->btd', o.reshape(B, T, D_MODEL), w_out)
    return y, (k, v, logf)


def mlstm_chunkwise(q, k, v, i_pre, logf, C0, n0, m0, block):
    B, L, H, _ = q.shape
    nb = L // block

    def to_blocks(a):
        a = a.reshape((B, nb, block) + a.shape[2:])
        return jnp.moveaxis(a, (1, 3), (0, 2))

    tri = jnp.tril(jnp.ones((block, block), dtype=bool))

    def step(carry, xs):
        C, n, m = carry
        qb, kb, vb, ib, fb = xs
        b = jnp.cumsum(fb, axis=-1)
        Dm = jnp.where(tri, b[..., :, None] - b[..., None, :] + ib[..., None, :], -jnp.inf)
        inter = b + m[..., None]
        mt = jnp.maximum(inter, jnp.max(Dm, axis=-1))
        w_inter = jnp.exp(inter - mt)
        S = jnp.einsum('bhtk,bhsk->bhts', qb, kb) * jnp.exp(Dm - mt[..., None])
        num = w_inter[..., None] * jnp.einsum('bhtk,bhvk->bhtv', qb, C) + jnp.einsum('bhts,bhsv->bhtv', S, vb)
        den = w_inter * jnp.einsum('bhtk,bhk->bht', qb, n) + jnp.sum(S, axis=-1)
        h = num / jnp.maximum(jnp.abs(den), jnp.exp(-mt))[..., None]
        m_new = mt[..., -1]
        g = b[..., -1:] - b + ib
        decay = jnp.exp(b[..., -1] + m - m_new)
        wg = jnp.exp(g - m_new[..., None])
        C_new = decay[..., None, None] * C + jnp.einsum('bhs,bhsv,bhsk->bhvk', wg, vb, kb)
        n_new = decay[..., None] * n + jnp.einsum('bhs,bhsk->bhk', wg, kb)
        return (C_new, n_new, m_new), h

    xs = (to_blocks(q), to_blocks(k), to_blocks(v), to_blocks(i_pre), to_blocks(logf))
    (C, n, m), hs = lax.scan(step, (C0, n0, m0), xs)
    h = jnp.moveaxis(hs, (0, 2), (1, 3)).reshape(B, L, H, hs.shape[-1])
    return h, C, n, m


def mlstm_mixer(h, w_in, b_i, b_f, g_h, w_out, C0, n0, m0, block, pad_front):
    B, T, _ = h.shape
    H, DK, DV = MLSTM_HEADS, MLSTM_DK, MLSTM_DV
    proj = jnp.einsum('btd,de->bte', h, w_in).astype(jnp.float32)
    q, k, v, o, ig, fg = jnp.split(proj, MLSTM_SPLITS, axis=-1)
    q = q.reshape(B, T, H, DK)
    k = k.reshape(B, T, H, DK) * (DK ** -0.5)
    v = v.reshape(B, T, H, DV)
    i_pre = ig + b_i.astype(jnp.float32)
    logf = jax.nn.log_sigmoid(fg + b_f.astype(jnp.float32))
    if pad_front:
        p4 = ((0, 0), (pad_front, 0), (0, 0), (0, 0))
        p3 = ((0, 0), (pad_front, 0), (0, 0))
        q, k, v = jnp.pad(q, p4), jnp.pad(k, p4), jnp.pad(v, p4)
        i_pre = jnp.pad(i_pre, p3, constant_values=NEG)
        logf = jnp.pad(logf, p3)
    hh, C, n, m = mlstm_chunkwise(q, k, v, i_pre, logf, C0.astype(jnp.float32),
                                  n0.astype(jnp.float32), m0.astype(jnp.float32), block)
    hh = rmsnorm(hh[:, pad_front:], g_h)
    o = jax.nn.sigmoid(o.reshape(B, T, H, DV))
    y = jnp.einsum('bte,ed->btd', (hh * o).reshape(B, T, H * DV).astype(h.dtype), w_out)
    return y, (C, n, m)


def run_trunk(x, fox_past, mlstm_init, mlstm_block, mlstm_pad,
              g_mix, g_ffn, fox_w_in, fox_b_f, fox_g_q, fox_g_k, fox_w_out,
              mlstm_w_in, mlstm_b_i, mlstm_b_f, mlstm_g_h, mlstm_w_out,
              ffn_w_up, ffn_w_down, g_final):
    fox_new, mlstm_new = [], []
    for i in range(DEPTH):
        h = rmsnorm(x, g_mix[i])
        j = i // N_MIXERS
        if i % N_MIXERS == 0:
            past = None if fox_past is None else (fox_past[0][j], fox_past[1][j], fox_past[2][j])
            y, st = fox_mixer(h, fox_w_in[j], fox_b_f[j], fox_g_q[j], fox_g_k[j], fox_w_out[j], past)
            fox_new.append(st)
        else:
            y, st = mlstm_mixer(h, mlstm_w_in[j], mlstm_b_i[j], mlstm_b_f[j], mlstm_g_h[j], mlstm_w_out[j],
                                mlstm_init[0][j], mlstm_init[1][j], mlstm_init[2][j], mlstm_block, mlstm_pad)
            mlstm_new.append(st)
        x = x + y
        x = x + sq_relu_mlp(rmsnorm(x, g_ffn[i]), ffn_w_up[i], ffn_w_down[i])
    out = rmsnorm(x, g_final)
    fk = jnp.stack([s[0] for s in fox_new])
    fv = jnp.stack([s[1] for s in fox_new])
    fl = jnp.stack([s[2] for s in fox_new])
    mC = jnp.stack([s[0] for s in mlstm_new])
    mn = jnp.stack([s[1] for s in mlstm_new])
    mm = jnp.stack([s[2] for s in mlstm_new])
    return out, fk, fv, fl, mC, mn, mm


def setup_inputs(seed: int = 0) -> dict:
    key = jax.random.key(seed)
    ks = jax.random.split(key, 26)
    f32 = jnp.float32

    def nrm(k, shape, scale):
        return jax.random.normal(k, shape, f32) * scale

    H, dh = FOX_HEADS, FOX_HEAD_DIM
    MH, DK, DV = MLSTM_HEADS, MLSTM_DK, MLSTM_DV
    return {
        'x_prompt': nrm(ks[0], (BATCH, SEQ, D_MODEL), 1.0),
        'x_sample': nrm(ks[1], (DEC_BATCH, DEC_SEQ, D_MODEL), 1.0),
        'cache_fox_k': nrm(ks[2], (N_FOX, DEC_BATCH, PAST_LEN, H, dh), 1.0),
        'cache_fox_v': nrm(ks[3], (N_FOX, DEC_BATCH, PAST_LEN, H, dh), 1.0),
        'cache_fox_logf': jax.nn.log_sigmoid(3.0 + nrm(ks[4], (N_FOX, DEC_BATCH, PAST_LEN, H), 1.0)),
        'state_mlstm_C': nrm(ks[5], (N_MLSTM, DEC_BATCH, MH, DV, DK), 0.1),
        'state_mlstm_n': nrm(ks[6], (N_MLSTM, DEC_BATCH, MH, DK), 0.1),
        'state_mlstm_m': nrm(ks[7], (N_MLSTM, DEC_BATCH, MH), 1.0),
        'meta_tokens': nrm(ks[8], (N_META, D_MODEL), 1.0),
        'g_mix': 1.0 + nrm(ks[9], (DEPTH, D_MODEL), 0.02),
        'g_ffn': 1.0 + nrm(ks[10], (DEPTH, D_MODEL), 0.02),
        'fox_w_in': nrm(ks[11], (N_FOX, D_MODEL, FOX_IN), D_MODEL ** -0.5),
        'fox_b_f': 3.0 + nrm(ks[12], (N_FOX, H), 0.1),
        'fox_g_q': 1.0 + nrm(ks[13], (N_FOX, dh), 0.02),
        'fox_g_k': 1.0 + nrm(ks[14], (N_FOX, dh), 0.02),
        'fox_w_out': nrm(ks[15], (N_FOX, D_MODEL, D_MODEL), D_MODEL ** -0.5),
        'mlstm_w_in': nrm(ks[16], (N_MLSTM, D_MODEL, MLSTM_IN), D_MODEL ** -0.5),
        'mlstm_b_i': nrm(ks[17], (N_MLSTM, MH), 0.1),
        'mlstm_b_f': 3.0 + nrm(ks[18], (N_MLSTM, MH), 0.1),
        'mlstm_g_h': 1.0 + nrm(ks[19], (N_MLSTM, MH, DV), 0.02),
        'mlstm_w_out': nrm(ks[20], (N_MLSTM, MH * DV, D_MODEL), (MH * DV) ** -0.5),
        'ffn_w_up': nrm(ks[21], (DEPTH, D_MODEL, D_FF), D_MODEL ** -0.5),
        'ffn_w_down': nrm(ks[22], (DEPTH, D_FF, D_MODEL), D_FF ** -0.5),
        'g_final': 1.0 + nrm(ks[23], (D_MODEL,), 0.02),
    }


def reference(x_prompt, x_sample, cache_fox_k, cache_fox_v, cache_fox_logf,
              state_mlstm_C, state_mlstm_n, state_mlstm_m, meta_tokens,
              g_mix, g_ffn, fox_w_in, fox_b_f, fox_g_q, fox_g_k, fox_w_out,
              mlstm_w_in, mlstm_b_i, mlstm_b_f, mlstm_g_h, mlstm_w_out,
              ffn_w_up, ffn_w_down, g_final):
    weights = (g_mix, g_ffn, fox_w_in, fox_b_f, fox_g_q, fox_g_k, fox_w_out,
               mlstm_w_in, mlstm_b_i, mlstm_b_f, mlstm_g_h, mlstm_w_out,
               ffn_w_up, ffn_w_down, g_final)

    B = x_prompt.shape[0]
    meta = jnp.broadcast_to(meta_tokens.astype(x_prompt.dtype)[None], (B, N_META, D_MODEL))
    xp = jnp.concatenate([meta, x_prompt], axis=1)
    init = (jnp.zeros((N_MLSTM, B, MLSTM_HEADS, MLSTM_DV, MLSTM_DK), jnp.float32),
            jnp.zeros((N_MLSTM, B, MLSTM_HEADS, MLSTM_DK), jnp.float32),
            jnp.zeros((N_MLSTM, B, MLSTM_HEADS), jnp.float32))
    yp, fk_p, fv_p, fl_p, mC_p, mn_p, mm_p = run_trunk(xp, None, init, CHUNK, MLSTM_PAD, *weights)
    y_prompt = yp[:, N_META:]

    T = x_sample.shape[1]
    ys, fk_s, fv_s, fl_s, mC_s, mn_s, mm_s = run_trunk(
        x_sample, (cache_fox_k, cache_fox_v, cache_fox_logf),
        (state_mlstm_C, state_mlstm_n, state_mlstm_m), T, 0, *weights)

    return (y_prompt, ys, fk_p, fv_p, fl_p, mC_p, mn_p, mm_p, fk_s, fv_s, fl_s, mC_s, mn_s, mm_s)
```

```python
import os
import contextlib
import numpy as np
import concourse.bass as bass
import concourse.mybir as mybir
from concourse.bass_utils import run_bass_kernel_spmd

F32 = mybir.dt.float32
BF16 = mybir.dt.bfloat16
AF = mybir.ActivationFunctionType
ALU = mybir.AluOpType
AX = mybir.AxisListType

ENGS = ("pe", "act", "dve", "pool", "sp")
NT = 34
R = NT * 128
T_SAMP = 32
T_META = 33
EPS = 1e-6


class Buf:
    __slots__ = ("name", "writers", "readers", "multi", "sem", "semval", "st_sem", "st_semval")

    def __init__(self, name, multi=False):
        self.name = name
        self.writers = []
        self.readers = []
        self.multi = multi
        self.sem = None
        self.semval = 0
        self.st_sem = None
        self.st_semval = 0


class Op:
    __slots__ = ("eng", "fn", "deps", "pos", "is_dma", "sem", "semval", "signal", "sigval")

    def __init__(self, eng, fn):
        self.eng = eng
        self.fn = fn
        self.deps = []
        self.pos = -1
        self.is_dma = False
        self.sem = None
        self.semval = 0
        self.signal = False
        self.sigval = 0


class Sched:
    def __init__(self, nc, same_engine_sync=True):
        self.nc = nc
        self.ops = {e: [] for e in ENGS}
        self.same_engine_sync = same_engine_sync
        self._sem_ctx = []
        self.bufs = []
        self.nsem = 0
        self.pre = []
        self.last_dma = {}
        self.sem_pool = []
        self.store_eng = None

    def new_sem(self, name):
        cm = self.nc.semaphore(name)
        h = cm.__enter__()
        self._sem_ctx.append(cm)
        self.nsem += 1
        return h

    def buf(self, name, multi=False):
        b = Buf(name, multi)
        b.readers = list(self.pre)
        self.bufs.append(b)
        return b

    def release_sems(self, bufs):
        for b in bufs:
            if b.sem is not None:
                self.sem_pool.append((b.sem, b.semval))
            if b.st_sem is not None:
                self.sem_pool.append((b.st_sem, b.st_semval))

    def phase_mark(self):
        self.pre = [self.ops[e][-1] for e in ('pe', 'act', 'dve', 'pool') if self.ops[e]] + list(self.last_dma.values())

    def _track(self, op, reads, writes):
        deps = []
        for b in reads:
            deps.extend(b.writers)
        for b in writes:
            if not b.multi:
                deps.extend(b.readers)
                deps.extend(b.writers)
        seen = set()
        for d in deps:
            if d is op or id(d) in seen:
                continue
            seen.add(id(d))
            op.deps.append(d)
        for b in reads:
            if not op.is_dma:
                b.readers = [r for r in b.readers if r.is_dma or r.eng != op.eng]
            b.readers.append(op)
        for b in writes:
            if b.multi:
                b.writers.append(op)
            else:
                b.writers = [op]
                b.readers = []

    def op(self, eng, fn, reads=(), writes=()):
        o = Op(eng, fn)
        o.pos = len(self.ops[eng])
        self.ops[eng].append(o)
        self._track(o, reads, writes)
        return o

    def dma(self, eng, pairs, src, dst, extra_reads=(), **kw):
        if self.store_eng is not None and (dst.multi or dst.name.startswith('D_')):
            eng = self.store_eng
        o = Op(eng, None)
        o.is_dma = True
        if not dst.multi:
            if dst.sem is None:
                if self.sem_pool:
                    dst.sem, dst.semval = self.sem_pool.pop()
                else:
                    dst.sem = self.new_sem("d_" + dst.name)
            dst.semval += 16 * len(pairs)
            o.sem, o.semval = dst.sem, dst.semval
        else:
            if src.st_sem is None:
                if self.sem_pool:
                    src.st_sem, src.st_semval = self.sem_pool.pop()
                else:
                    src.st_sem = self.new_sem("s_" + src.name)
            src.st_semval += 16 * len(pairs)
            o.sem, o.semval = src.st_sem, src.st_semval
        sem = o.sem
        self.last_dma[id(sem)] = o

        def fn(e, pairs=pairs, sem=sem, kw=kw):
            for (out_ap, in_ap) in pairs:
                e.dma_start(out=out_ap, in_=in_ap, **kw).then_inc(sem, 16)
        o.fn = fn
        o.pos = len(self.ops[eng])
        self.ops[eng].append(o)
        self._track(o, [src] + list(extra_reads), [dst])
        return o

    def emit(self):
        nc = self.nc
        for e in ENGS:
            for o in self.ops[e]:
                for d in o.deps:
                    if not d.is_dma:
                        if d.eng == o.eng and (d.eng == "pe" or not self.same_engine_sync):
                            continue
                        d.signal = True
        esem = {}
        for e in ENGS:
            if any(o.signal for o in self.ops[e]):
                esem[e] = self.new_sem("e_" + e)
            c = 0
            for o in self.ops[e]:
                if o.signal and not o.is_dma:
                    c += 1
                    o.sigval = c
        final = {}
        for b in self.bufs:
            if b.st_sem is not None:
                final[b.st_sem] = max(final.get(b.st_sem, 0), b.st_semval)
            if b.sem is not None:
                final[b.sem] = max(final.get(b.sem, 0), b.semval)
        handles = {"pe": "tensor", "act": "scalar", "dve": "vector", "pool": "gpsimd", "sp": "sync"}
        stats = {e: [len(self.ops[e]), 0] for e in ENGS}
        same = self.same_engine_sync

        def run_engine(ename, eh):
            waited = {}
            for o in self.ops[ename]:
                for d in o.deps:
                    if d.is_dma:
                        s, v = d.sem, d.semval
                    else:
                        if d.eng == ename and (ename == "pe" or not same):
                            continue
                        s, v = esem[d.eng], d.sigval
                    if waited.get(s, 0) >= v:
                        continue
                    waited[s] = v
                    eh.wait_ge(s, v)
                    stats[ename][1] += 1
                if o.is_dma:
                    o.fn(eh)
                else:
                    ins = o.fn(eh)
                    if o.signal:
                        ins.then_inc(esem[ename], 1)
            if ename == "sp":
                for s, v in final.items():
                    if waited.get(s, 0) < v:
                        eh.wait_ge(s, v)

        with nc.Block() as block:
            for ename in ENGS:
                deco = getattr(block, handles[ename])

                def body(eh, ename=ename):
                    run_engine(ename, eh)
                deco(body)
        for cm in reversed(self._sem_ctx):
            cm.__exit__(None, None, None)
        return stats


class RR:
    def __init__(self, items):
        self.items = items
        self.i = 0

    def next(self):
        it = self.items[self.i % len(self.items)]
        self.i += 1
        return it


def build_program(FT=32, PB=16):
    NT = FT + 2
    R = NT * 128
    TS = FT
    TM = FT + 1
    PH = os.environ.get("PH", "ABCDEFGH")
    DEBUG = os.environ.get("KDEBUG", "") == "1"
    nc = bass.Bass("TRN2", target_bir_lowering=False)
    es = contextlib.ExitStack()
    S = Sched(nc, same_engine_sync=(os.environ.get('SES', '1') == '1'))
    S.store_eng = os.environ.get('STQ', 'pool') or None

    def din(name, shape, dt=F32):
        return nc.dram_tensor(name, list(shape), dt, kind="ExternalInput").ap()

    def dout(name, shape, dt=F32):
        return nc.dram_tensor(name, list(shape), dt, kind="ExternalOutput").ap()

    def dscr(name, shape, dt):
        return nc.dram_tensor(name, list(shape), dt, kind="Internal").ap()

    cur = [es]

    uid = [0]

    def sb(name, shape, dt):
        uid[0] += 1
        name = "%s_%d" % (name, uid[0])
        t = cur[0].enter_context(nc.sbuf_tensor(name, list(shape), dt))
        return t, S.buf(name)

    def run_phase(fn, *a):
        if os.environ.get('NOLOCAL') == '1' and fn.__name__ == 'phase_FFN':
            fn(*a)
            return
        n0 = len(S.bufs)
        with contextlib.ExitStack() as pes:
            cur[0] = pes
            fn(*a)
            cur[0] = es
        S.phase_mark()
        S.release_sems(S.bufs[n0:])

    def sbs(name, shape, dt, n):
        return RR([sb("%s%d" % (name, i), shape, dt) for i in range(n)])

    def pool(fn, reads, writes):
        return S.op("pool", fn, reads, writes)

    def dve(fn, reads, writes):
        return S.op("dve", fn, reads, writes)

    def act(fn, reads, writes):
        return S.op("act", fn, reads, writes)

    def pe(fn, reads, writes):
        return S.op("pe", fn, reads, writes)

    xin = din("xin", [R, 1024])
    ck = din("ck", [4, PB * 128, 1024]); cv = din("cv", [4, PB * 128, 1024]); cl = din("cl", [4, PB * 128, 16])
    mC0 = din("mC0", [4, 4, 256, 128]); mn0 = din("mn0", [4, 4, 128]); mm0 = din("mm0", [4, 4])
    g_mix = din("g_mix", [2, 1024]); g_ffn = din("g_ffn", [2, 1024]); g_final = din("g_final", [1, 1024])
    fox_w_in = din("fox_w_in", [1024, 3088]); fox_b_f = din("fox_b_f", [1, 16])
    fox_g_q = din("fox_g_q", [1, 64]); fox_g_k = din("fox_g_k", [1, 64])
    fox_w_out = din("fox_w_out", [1024, 1024])
    ml_w_in = din("ml_w_in", [1024, 3080]); ml_b_i = din("ml_b_i", [1, 4]); ml_b_f = din("ml_b_f", [1, 4])
    ml_g_h = din("ml_g_h", [1, 1024]); ml_w_out = din("ml_w_out", [1024, 1024])
    w_up = din("w_up", [2, 1024, 4096]); w_down = din("w_down", [2, 4096, 1024])

    y_out = dout("y_out", [R, 1024])
    k_out = dout("k_out", [R, 1024]); v_out = dout("v_out", [R, 1024]); lf_out = dout("lf_out", [R, 16])
    C_out = dout("C_out", [5, 4, 256, 128]); n_out = dout("n_out", [5, 4, 128]); m_out = dout("m_out", [5, 4])

    D_in = S.buf("D_in", multi=True)
    D_out = S.buf("D_out", multi=True)

    xs_d = dscr("xs", [R, 1024], F32)
    D_xs = [S.buf("D_xs%d" % i) for i in range(NT)]
    kT_d = dscr("kT", [8, 128, R], BF16); D_kT = S.buf("D_kT", multi=True)
    qT_d = dscr("qT", [8, 128, R], BF16); D_qT = S.buf("D_qT", multi=True)
    v2_d = dscr("v2", [NT, 128, 2048], BF16); D_v2 = S.buf("D_v2", multi=True)
    oT_d = dscr("oT", [8, 128, R], BF16); D_oT = S.buf("D_oT", multi=True)
    kcT_d = dscr("kcT", [4, 8, 128, PB * 128], BF16); D_kcT = S.buf("D_kcT", multi=True)
    v2c_d = dscr("v2c", [4, PB, 128, 2048], BF16); D_v2c = S.buf("D_v2c", multi=True)

    PS = []
    for i in range(8):
        t = es.enter_context(nc.psum_tensor("ps%d" % i, [128, 512], F32))
        PS.append((t, S.buf("ps%d" % i)))
    psrr = RR(PS)

    ident, Bident = sb("ident", [128, 128], F32)
    identb, Bidentb = sb("identb", [128, 128], BF16)
    tri, Btri = sb("tri", [128, 128], F32)
    triBD, BtriBD = sb("triBD", [128, 128], F32)
    ones, Bones = sb("ones", [128, 128], F32)
    onehot0, Bonehot0 = sb("onehot0", [128, 128], F32)
    maskn, Bmaskn = sb("maskn", [128, 128], BF16)
    masknBD, BmasknBD = sb("masknBD", [128, 128], BF16)
    ones2e, Bones2e = sb("ones2e", [128, 128], BF16)
    ones2o, Bones2o = sb("ones2o", [128, 128], BF16)
    upper, Bupper = sb("upper", [128, 128], F32)
    vmask, Bvmask = sb("vmask", [128, 2], F32)

    pool(lambda e: e.memset(ident[:], 0.0), [], [Bident])
    pool(lambda e: e.affine_select(out=ident[:], in_=ident[:], pattern=[[-1, 128]], compare_op=ALU.not_equal,
                                   fill=1.0, base=0, channel_multiplier=1), [Bident], [Bident])
    pool(lambda e: e.tensor_copy(out=identb[:], in_=ident[:]), [Bident], [Bidentb])
    pool(lambda e: e.memset(ones[:], 1.0), [], [Bones])
    pool(lambda e: e.memset(onehot0[:], 0.0), [], [Bonehot0])
    pool(lambda e: e.memset(onehot0[0:1, :], 1.0), [Bonehot0], [Bonehot0])
    pool(lambda e: e.affine_select(out=tri[:], in_=ones[:], pattern=[[1, 128]], compare_op=ALU.is_ge,
                                   fill=0.0, base=0, channel_multiplier=-1), [Bones], [Btri])
    pool(lambda e: e.affine_select(out=upper[:], in_=ones[:], pattern=[[-1, 128]], compare_op=ALU.is_gt,
                                   fill=0.0, base=0, channel_multiplier=1), [Bones], [Bupper])
    pool(lambda e: e.tensor_copy(out=triBD[:], in_=tri[:]), [Btri], [BtriBD])
    pool(lambda e: e.memset(maskn[:], 0.0), [], [Bmaskn])
    pool(lambda e: e.affine_select(out=maskn[:], in_=maskn[:], pattern=[[1, 128]], compare_op=ALU.is_ge,
                                   fill=-30000.0, base=0, channel_multiplier=-1), [Bmaskn], [Bmaskn])
    pool(lambda e: e.tensor_copy(out=masknBD[:], in_=maskn[:]), [Bmaskn], [BmasknBD])
    for i in range(1, 4):
        def f1(e, i=i):
            return e.affine_select(out=triBD[:, 32 * i:32 * i + 32], in_=triBD[:, 32 * i:32 * i + 32],
                                   pattern=[[0, 32]], compare_op=ALU.is_ge, fill=0.0, base=-32 * i,
                                   channel_multiplier=1)
        pool(f1, [BtriBD], [BtriBD])

        def f2(e, i=i):
            return e.affine_select(out=masknBD[:, 32 * i:32 * i + 32], in_=masknBD[:, 32 * i:32 * i + 32],
                                   pattern=[[0, 32]], compare_op=ALU.is_ge, fill=-30000.0, base=-32 * i,
                                   channel_multiplier=1)
        pool(f2, [BmasknBD], [BmasknBD])
    pool(lambda e: e.memset(ones2e[:], 0.0), [], [Bones2e])
    pool(lambda e: e.memset(ones2e[:, 0:64], 1.0), [Bones2e], [Bones2e])
    pool(lambda e: e.memset(ones2o[:], 0.0), [], [Bones2o])
    pool(lambda e: e.memset(ones2o[:, 64:128], 1.0), [Bones2o], [Bones2o])
    pool(lambda e: e.memset(vmask[:, 0:1], 0.0), [], [Bvmask])
    pool(lambda e: e.memset(vmask[0:16, 0:1], 1.0), [Bvmask], [Bvmask])
    pool(lambda e: e.memset(vmask[:, 1:2], -1e30), [Bvmask], [Bvmask])
    pool(lambda e: e.memset(vmask[0:16, 1:2], 0.0), [Bvmask], [Bvmask])

    grep, Bgrep = sb("grep", [128, 1024], F32)
    gk_rep, Bgk_rep = sb("gk_rep", [128, 64], F32)
    gq2, Bgq2 = sb("gq2", [128, 1], F32)
    gk2, Bgk2 = sb("gk2", [128, 1], F32)
    bf_rep, Bbf_rep = sb("bf_rep", [128, 16], F32)
    S.dma("sp", [(gk_rep[:], fox_g_k[0, :].partition_broadcast(128))], D_in, Bgk_rep)
    S.dma("sp", [(gq2[0:64, :], fox_g_q[0, :].rearrange("(d o) -> d o", o=1)),
                 (gq2[64:128, :], fox_g_q[0, :].rearrange("(d o) -> d o", o=1))], D_in, Bgq2)
    S.dma("sp", [(gk2[0:64, :], fox_g_k[0, :].rearrange("(d o) -> d o", o=1)),
                 (gk2[64:128, :], fox_g_k[0, :].rearrange("(d o) -> d o", o=1))], D_in, Bgk2)
    S.dma("sp", [(bf_rep[:], fox_b_f[0, :].partition_broadcast(128))], D_in, Bbf_rep)

    WK = []
    for kc in range(8):
        t_ = es.enter_context(nc.sbuf_tensor("WK%d" % kc, [128, 4096], BF16))
        WK.append((t_, S.buf("WKa%d" % kc), S.buf("WKb%d" % kc)))

    def wkb(kc, c0, n):
        bs = []
        if c0 < 2048:
            bs.append(WK[kc][1])
        if c0 + n > 2048:
            bs.append(WK[kc][2])
        return bs

    def load_piece(src_ap, ncols, kc, dcol0):
        c = 0
        while c < ncols:
            d0 = dcol0 + c
            n = min(ncols - c, (2048 - d0) if d0 < 2048 else (4096 - d0))
            S.dma("pool", [(WK[kc][0][:, d0:d0 + n], src_ap[:, c:c + n])], D_in, wkb(kc, d0, n)[0], max_dma_last_dim=4096)
            c += n

    def load_w(w_ap, ncols, col0=0, dcol0=0):
        for kc in range(8):
            load_piece(w_ap[kc * 128:(kc + 1) * 128, col0:col0 + ncols], ncols, kc, dcol0)

    def load_w_down_half(w_ap, half):
        for fc in range(16):
            r0 = half * 2048 + fc * 128
            load_piece(w_ap[r0:r0 + 128, :], 1024, fc // 2, 2048 + (fc % 2) * 1024)

    def load_grep(src_row):
        S.dma("sp", [(grep[:], src_row.partition_broadcast(128))], D_in, Bgrep)

    xt_rr = sbs("xt", [128, 1024], F32, 3)
    junk, Bjunk = sb("junk", [128, 1024], BF16)
    hb_rr = sbs("hb", [128, 1024], F32, 2)
    hT_rr = sbs("hT", [128, 1024], BF16, 2)
    st_rr = sbs("stat", [128, 8], F32, 4)

    def rstd_from_ss(ss_ap, Bss, mean_div):
        dve(lambda e: e.tensor_scalar(out=ss_ap, in0=ss_ap, scalar1=1.0 / mean_div, scalar2=EPS,
                                      op0=ALU.mult, op1=ALU.add), [Bss], [Bss])
        act(lambda e: e.activation(out=ss_ap, in_=ss_ap, func=AF.Sqrt), [Bss], [Bss])
        dve(lambda e: e.reciprocal(out=ss_ap, in_=ss_ap), [Bss], [Bss])

    def rmsnorm_tile(x, Bx, out, Bout):
        st, Bst = st_rr.next()
        act(lambda e: e.activation(out=junk[:], in_=x, func=AF.Square, accum_out=st[:, 0:1]), [Bx], [Bjunk, Bst])
        rstd_from_ss(st[:, 0:1], Bst, 1024.0)
        dve(lambda e: e.scalar_tensor_tensor(out=out, in0=x, scalar=st[:, 0:1], in1=grep[:],
                                             op0=ALU.mult, op1=ALU.mult), [Bx, Bst, Bgrep], [Bout])

    def transpose_1024(src, Bsrc, dst_of_group, Bdst):
        for g in range(2):
            pt, Bpt = psrr.next()
            for j in range(4):
                kc = g * 4 + j
                pe(lambda e, kc=kc, j=j, pt=pt: e.transpose(out=pt[:, j * 128:(j + 1) * 128],
                                                            in_=src[:, kc * 128:(kc + 1) * 128], identity=ident[:]),
                   [Bsrc, Bident], [Bpt])
            o_ap = dst_of_group(g)
            i_ap = pt[:, 0:512] if len(o_ap.shape) == 2 else pt[:, 0:512].rearrange("p (j t) -> p j t", j=4)
            act(lambda e, o_ap=o_ap, i_ap=i_ap: e.activation(out=o_ap, in_=i_ap, func=AF.Copy), [Bpt], [Bdst])

    hbb_rr = sbs("hbb", [128, 1024], BF16, 2)

    def transpose_bf(src, Bsrc, dst_ap, Bdst):
        pt, Bpt = psrr.next()
        ptb = pt[:, 0:512].bitcast(BF16)
        for kc in range(8):
            pe(lambda e, kc=kc: e.transpose(out=ptb[:, kc * 128:(kc + 1) * 128], in_=src[:, kc * 128:(kc + 1) * 128], identity=identb[:]),
               [Bsrc, Bidentb], [Bpt])
        i_ap = ptb if len(dst_ap.shape) == 2 else ptb.rearrange("p (k t) -> p k t", k=8)
        act(lambda e: e.activation(out=dst_ap, in_=i_ap, func=AF.Copy), [Bpt], [Bdst])

    def norm_and_transpose(x, Bx):
        hb, Bhb = hbb_rr.next()
        rmsnorm_tile(x[:], Bx, hb[:], Bhb)
        hT, BhT = hT_rr.next()
        transpose_bf(hb, Bhb, hT[:], BhT)
        return hT, BhT

    def mm_group(hT, BhT, c0, ncols):
        p, Bp = psrr.next()
        for kc in range(8):
            pe(lambda e, kc=kc: e.matmul(p[:, 0:ncols], lhsT=hT[:, kc * 128:(kc + 1) * 128],
                                         rhs=WK[kc][0][:, c0:c0 + ncols], start=(kc == 0), stop=(kc == 7)),
               [BhT] + wkb(kc, c0, ncols), [Bp])
        return p, Bp

    def run_interleaved(gens, lag):
        gens = list(gens)
        active = []
        while gens or active:
            if gens and len(active) < 2 and (not active or active[-1][1] >= lag):
                active.append([gens.pop(0), 0])
            for a in list(active):
                try:
                    next(a[0])
                    a[1] += 1
                except StopIteration:
                    active.remove(a)

    def make_prefetcher(order, loader):
        order = list(order)
        cache = {}

        def get(ti):
            if ti not in cache:
                cache[ti] = loader(ti)
            h = cache.pop(ti)
            k = order.index(ti)
            if k + 1 < len(order) and order[k + 1] not in cache:
                cache[order[k + 1]] = loader(order[k + 1])
            return h
        return get

    def load_x(ti):
        x, Bx = xt_rr.next()
        S.dma("sp", [(x[:], xs_d[ti * 128:(ti + 1) * 128, :])], D_xs[ti], Bx)
        return x, Bx

    def store_x(ti, x, Bx):
        S.dma("sp", [(xs_d[ti * 128:(ti + 1) * 128, :], x[:])], Bx, D_xs[ti])

    nF, BnF = sb("nF", [128, NT * 16], F32)
    car, Bcar = sb("car", [128, 16], F32)
    revb, Brevb = sb("revb", [128, 4 * PB * 16], F32)
    tile_order = [TM] + list(range(FT)) + [TS]

    def phase_A():
        load_w(fox_w_in, 3088)
        load_grep(g_mix[0, :])
        pool(lambda e: e.memset(car[:], 0.0), [], [Bcar])
        sq_rr = sbs("sq", [128, 512], F32, 2)
        kn_rr = sbs("kn", [128, 512], F32, 2)
        kf_rr = sbs("kf", [128, 1024], F32, 2)
        vf_rr = sbs("vf", [128, 1024], F32, 2)
        v2_rr = sbs("v2t", [128, 2048], BF16, 2)
        for (t_, B_) in v2_rr.items:
            pool(lambda e, t_=t_: e.memset(t_[:], 0.0), [], [B_])
        tT_rr = sbs("tT", [128, 512], BF16, 3)
        lf_rr = sbs("lf", [128, 16], F32, 2)
        z_rr = sbs("z", [128, 16], F32, 2)

        def qk_epilogue(p, Bp, half, ti, is_k, kf, Bkf):
            sq, Bsq = sq_rr.next()
            st, Bst = st_rr.next()
            kn, Bkn = kn_rr.next()
            act(lambda e: e.activation(out=sq[:], in_=p[:, 0:512], func=AF.Square), [Bp], [Bsq])
            dve(lambda e: e.tensor_reduce(out=st[:, 0:8], in_=sq[:].rearrange("p (h d) -> p h d", h=8),
                                          axis=AX.X, op=ALU.add), [Bsq], [Bst])
            rstd_from_ss(st[:, 0:8], Bst, 64.0)
            dve(lambda e: e.tensor_tensor(out=kn[:].rearrange("p (h d) -> p h d", h=8),
                                          in0=p[:, 0:512].rearrange("p (h d) -> p h d", h=8),
                                          in1=st[:, 0:8].unsqueeze(2).to_broadcast([128, 8, 64]), op=ALU.mult),
                [Bp, Bst], [Bkn])
            if is_k:
                dve(lambda e: e.tensor_tensor(out=kf[:, half * 512:(half + 1) * 512].rearrange("p (h d) -> p h d", h=8),
                                              in0=kn[:].rearrange("p (h d) -> p h d", h=8),
                                              in1=gk_rep[:].unsqueeze(1).to_broadcast([128, 8, 64]), op=ALU.mult),
                    [Bkn, Bgk_rep], [Bkf])
            yield
            pt, Bpt = psrr.next()
            for j in range(4):
                pe(lambda e, j=j: e.transpose(out=pt[:, j * 128:(j + 1) * 128], in_=kn[:, j * 128:(j + 1) * 128],
                                              identity=ident[:]), [Bkn, Bident], [Bpt])
            tT, BtT = tT_rr.next()
            g2, Bg2 = (gk2, Bgk2) if is_k else (gq2, Bgq2)
            act(lambda e: e.activation(out=tT[:], in_=pt[:, 0:512], func=AF.Copy, scale=g2[:, 0:1]), [Bpt, Bg2], [BtT])
            dst_d, Bdst = (kT_d, D_kT) if is_k else (qT_d, D_qT)
            S.dma("sp", [(dst_d[half * 4:(half + 1) * 4, :, ti * 128:(ti + 1) * 128].rearrange("j p t -> p j t"),
                          tT[:].rearrange("p (j t) -> p j t", j=4))], BtT, Bdst)

        def tileA(ti):
            x, Bx = getA(ti)
            store_x(ti, x, Bx)
            hT, BhT = norm_and_transpose(x, Bx)
            yield
            SK = os.environ.get('SK', '')
            kf, Bkf = kf_rr.next()
            for half in range(2):
                if 'q' in SK:
                    continue
                p, Bp = mm_group(hT, BhT, half * 512, 512)
                yield
                yield from qk_epilogue(p, Bp, half, ti, False, None, None)
                yield
            for half in range(2):
                if 'k' in SK:
                    continue
                p, Bp = mm_group(hT, BhT, 1024 + half * 512, 512)
                yield
                yield from qk_epilogue(p, Bp, half, ti, True, kf, Bkf)
                yield
            if 'k' not in SK:
                S.dma("sp", [(k_out[ti * 128:(ti + 1) * 128, :], kf[:])], Bkf, D_out)
            vf, Bvf = vf_rr.next()
            v2, Bv2 = v2_rr.next()
            for half in range(2):
                if 'v' in SK:
                    continue
                yield
                p, Bp = mm_group(hT, BhT, 2048 + half * 512, 512)
                act(lambda e, half=half, p=p: e.activation(out=vf[:, half * 512:(half + 1) * 512], in_=p[:, 0:512], func=AF.Copy),
                    [Bp], [Bvf])
                v2v = v2[:].rearrange("p (h c) -> p h c", h=16)
                pv = p[:, 0:512].rearrange("p (h two d) -> p h two d", two=2, d=64)
                act(lambda e, half=half, v2v=v2v, pv=pv: e.activation(out=v2v[:, half * 8:half * 8 + 8:2, 0:64], in_=pv[:, :, 0, :], func=AF.Copy),
                    [Bp], [Bv2])
                act(lambda e, half=half, v2v=v2v, pv=pv: e.activation(out=v2v[:, half * 8 + 1:half * 8 + 8:2, 64:128], in_=pv[:, :, 1, :], func=AF.Copy),
                    [Bp], [Bv2])
            if 'v' not in SK:
                S.dma("sp", [(v_out[ti * 128:(ti + 1) * 128, :], vf[:])], Bvf, D_out)
                S.dma("sp", [(v2_d[ti, :, :], v2[:])], Bv2, D_v2)
            if 'f' in SK:
                return
            yield
            p, Bp = mm_group(hT, BhT, 3072, 16)
            z, Bz = z_rr.next()
            lf, Blf = lf_rr.next()
            dve(lambda e: e.tensor_tensor(out=z[:], in0=p[:, 0:16], in1=bf_rep[:], op=ALU.add), [Bp, Bbf_rep], [Bz])
            act(lambda e: e.activation(out=z[:], in_=z[:], func=AF.Exp, scale=-1.0), [Bz], [Bz])
            act(lambda e: e.activation(out=z[:], in_=z[:], func=AF.Ln, bias=1.0), [Bz], [Bz])
            dve(lambda e: e.tensor_scalar(out=lf[:], in0=z[:], scalar1=-1.0, scalar2=None, op0=ALU.mult), [Bz], [Blf])
            S.dma("sp", [(lf_out[ti * 128:(ti + 1) * 128, :], lf[:])], Blf, D_out)
            if ti == TM:
                dve(lambda e: e.tensor_scalar(out=z[:], in0=lf[:], scalar1=vmask[:, 0:1], scalar2=None, op0=ALU.mult),
                    [Blf, Bvmask], [Bz])
                lfc, Blfc = z, Bz
            else:
                lfc, Blfc = lf, Blf
            pf, Bpf = psrr.next()
            if ti == TS:
                pe(lambda e: e.matmul(pf[:, 0:16], lhsT=triBD[:], rhs=lfc[:], start=True, stop=True), [BtriBD, Blfc], [Bpf])
                dve(lambda e: e.tensor_scalar(out=nF[:, ti * 16:(ti + 1) * 16], in0=pf[:, 0:16], scalar1=-1.0, scalar2=None,
                                              op0=ALU.mult), [Bpf], [BnF])
            else:
                pe(lambda e: e.matmul(pf[:, 0:16], lhsT=tri[:], rhs=lfc[:], start=True, stop=False), [Btri, Blfc], [Bpf])
                pe(lambda e: e.matmul(pf[:, 16:32], lhsT=ones[:], rhs=lfc[:], start=False, stop=True), [Bones, Blfc], [Bpf])
                dve(lambda e: e.scalar_tensor_tensor(out=nF[:, ti * 16:(ti + 1) * 16], in0=pf[:, 0:16], scalar=-1.0,
                                                     in1=car[:], op0=ALU.mult, op1=ALU.subtract), [Bpf, Bcar], [BnF])
                dve(lambda e: e.tensor_tensor(out=car[:], in0=pf[:, 16:32], in1=car[:], op=ALU.add), [Bpf, Bcar], [Bcar])

        def loadA(ti):
            x, Bx = xt_rr.next()
            S.dma("sp", [(x[:], xin[ti * 128:(ti + 1) * 128, :])], D_in, Bx)
            return x, Bx
        getA = make_prefetcher(tile_order, loadA)
        run_interleaved([tileA(ti) for ti in tile_order], lag=8)

        clall, Bclall = sb("clall", [128, 4 * PB * 16], F32)
        clv = clall[:].rearrange("p (i b h) -> p i b h", i=4, b=PB)
        S.dma("sp", [(clv[:, i, :, :], cl[i].rearrange("(b p) h -> p b h", p=128)) for i in range(4)], D_in, Bclall)
        revv = revb[:].rearrange("p (i b h) -> p i b h", i=4, b=PB)
        ckt_rr = sbs("ckt", [128, 1024], F32, 2)
        cvt_rr = sbs("cvt", [128, 1024], F32, 2)
        kcs_rr = sbs("kcs", [128, 1024], BF16, 2)

        def cache_block(i, b):
            ckt, Bckt = ckt_rr.next()
            cvt, Bcvt = cvt_rr.next()
            S.dma("sp", [(ckt[:], ck[i, b * 128:(b + 1) * 128, :])], D_in, Bckt)
            S.dma("sp", [(cvt[:], cv[i, b * 128:(b + 1) * 128, :])], D_in, Bcvt)
            yield
            kcs, Bkcs = kcs_rr.next()
            transpose_1024(ckt, Bckt, lambda g: kcs[:, g * 512:(g + 1) * 512], Bkcs)
            S.dma("sp", [(kcT_d[i, :, :, b * 128:(b + 1) * 128].rearrange("j p t -> p j t"),
                          kcs[:].rearrange("p (j t) -> p j t", j=8))], Bkcs, D_kcT)
            yield
            v2, Bv2 = v2_rr.next()
            v2v = v2[:].rearrange("p (h c) -> p h c", h=16)
            cvv = cvt[:].rearrange("p (h two d) -> p h two d", two=2, d=64)
            dve(lambda e: e.tensor_copy(out=v2v[:, 0:16:2, 0:64], in_=cvv[:, :, 0, :]), [Bcvt], [Bv2])
            dve(lambda e: e.tensor_copy(out=v2v[:, 1:16:2, 64:128], in_=cvv[:, :, 1, :]), [Bcvt], [Bv2])
            S.dma("sp", [(v2c_d[i, b, :, :], v2[:])], Bv2, D_v2c)

        def rev_bias(i):
            rc, Brc = sb("rcar%d" % i, [128, 16], F32)
            pool(lambda e: e.memset(rc[:], 0.0), [], [Brc])
            for b in reversed(range(PB)):
                pf, Bpf = psrr.next()
                pe(lambda e, b=b, pf=pf: e.matmul(pf[:, 0:16], lhsT=upper[:], rhs=clv[:, i, b, :], start=True, stop=False), [Bupper, Bclall], [Bpf])
                pe(lambda e, b=b, pf=pf: e.matmul(pf[:, 16:32], lhsT=ones[:], rhs=clv[:, i, b, :], start=False, stop=True), [Bones, Bclall], [Bpf])
                dve(lambda e, b=b, pf=pf: e.tensor_tensor(out=revv[:, i, b, :], in0=pf[:, 0:16], in1=rc[:], op=ALU.add), [Bpf, Brc], [Brevb])
                dve(lambda e, pf=pf: e.tensor_tensor(out=rc[:], in0=pf[:, 16:32], in1=rc[:], op=ALU.add), [Bpf, Brc], [Brc])
        for i in range(4):
            rev_bias(i)

        run_interleaved([cache_block(i, b_) for i in range(4) for b_ in range(PB)], lag=2)

    if "A" in PH:
        run_phase(phase_A)

    def phase_B():
        nFv = nF[:].rearrange("p (t h) -> p t h", h=16)
        revv = revb[:].rearrange("p (i b h) -> p i b h", i=4, b=PB)
        Fr, BFr = sb("Fr", [128, FT * 16], F32)
        pfr, Bpfr = psrr.next()
        for qi in range(FT):
            pe(lambda e, qi=qi: e.matmul(pfr[:, qi * 16:(qi + 1) * 16], lhsT=onehot0[:], rhs=nFv[:, qi, :],
                                         start=(qi == 0), stop=(qi == FT - 1)), [Bonehot0, BnF], [Bpfr])
        act(lambda e: e.activation(out=Fr[:], in_=pfr[:, 0:FT * 16], func=AF.Copy), [Bpfr], [BFr])
        Frv = Fr[:].rearrange("p (t h) -> p t h", h=16)

        KT, BKT = sb("KTp", [128, R], BF16)
        QT, BQT = sb("QTp", [128, R], BF16)
        V2p, BV2p = sb("V2p", [128, NT * 256], BF16)
        V2v = V2p[:].rearrange("p (t c) -> p t c", c=256)
        KcT_rr = sbs("KcT", [128, PB * 128], BF16, 3)
        V2c_rr = sbs("V2c", [128, PB * 256], BF16, 3)
        bias_rr = sbs("biasq", [128, NT * 16], F32, 2)
        SQ = min(4, FT)
        pt_rr = sbs("ptb", [128, 512], BF16, 4)
        rden_rr = sbs("rden", [128, 512], F32, 2)
        oTt_rr = sbs("oTt", [128, 512], BF16, 2)
        ones2 = [(ones2e, Bones2e), (ones2o, Bones2o)]
        acc_rr = RR(PS[0:4])
        sc_rr = RR(PS[4:8])

        items = []
        LA = 2

        def block(p, e_, qcol0, nq, kT_ap, nk, v_ap, bias_ap, Bbias, mask, Bmask, o_ps, Bo, d_ps, Bd, ocol0, first, last, Bk, Bv, pre=None):
            r0 = e_ * 64
            st = {}

            def stage1():
                if pre is not None:
                    pre()
                ps_s, Bs = sc_rr.next()
                pe(lambda e: e.matmul(ps_s[0:nk, 0:nq], lhsT=kT_ap, rhs=QT[r0:r0 + 64, qcol0:qcol0 + nq],
                                      start=True, stop=(mask is None)), [Bk, BQT], [Bs])
                if mask is not None:
                    mq = min(nq, 128)
                    pe(lambda e: e.matmul(ps_s[0:nk, 0:mq], lhsT=identb[0:nk, 0:nk], rhs=mask[0:nk, 0:mq],
                                          start=False, stop=True), [Bidentb, Bmask], [Bs])
                ptb, Bptb = pt_rr.next()
                act(lambda e: e.activation(out=ptb[0:nk, 0:nq], in_=ps_s[0:nk, 0:nq], func=AF.Exp, bias=bias_ap, scale=0.125),
                    [Bs, Bbias], [Bptb])
                st["pt"] = (ptb, Bptb)

            def stage2():
                ptb, Bptb = st["pt"]
                o2, Bo2 = ones2[e_]
                pe(lambda e: e.matmul(o_ps[:, ocol0:ocol0 + nq], lhsT=v_ap, rhs=ptb[0:nk, 0:nq], start=first, stop=last),
                   [Bv, Bptb], [Bo])
                pe(lambda e: e.matmul(d_ps[:, ocol0:ocol0 + nq], lhsT=o2[0:nk, :], rhs=ptb[0:nk, 0:nq], start=first, stop=last),
                   [Bo2, Bptb], [Bd])
            items.append([stage1, stage2, None])

        def flush():
            n = len(items)
            for i in range(n + LA):
                if i < n:
                    items[i][0]()
                if i - LA >= 0:
                    items[i - LA][1]()
                    if items[i - LA][2] is not None:
                        items[i - LA][2]()
            del items[:]

        def finish(p, q0, W, o_ps, Bo, d_ps, Bd):
            items[-1][2] = lambda: finish_now(p, q0, W, o_ps, Bo, d_ps, Bd)

        def finish_now(p, q0, W, o_ps, Bo, d_ps, Bd):
            rden, Brden = rden_rr.next()
            oTt, BoTt = oTt_rr.next()
            dve(lambda e: e.reciprocal(out=rden[:, 0:W], in_=d_ps[:, 0:W]), [Bd], [Brden])
            dve(lambda e: e.tensor_tensor(out=oTt[:, 0:W], in0=o_ps[:, 0:W], in1=rden[:, 0:W], op=ALU.mult), [Bo, Brden], [BoTt])
            S.dma("sp", [(oT_d[p, :, q0:q0 + W], oTt[:, 0:W])], BoTt, D_oT)

        for p in range(8):
            S.dma("sp", [(KT[:], kT_d[p, :, :])], D_kT, BKT)
            S.dma("sp", [(QT[:], qT_d[p, :, :])], D_qT, BQT)
            S.dma("sp", [(V2v, v2_d[:, :, p * 256:(p + 1) * 256].rearrange("t s c -> s t c"))], D_v2, BV2p)
            units = []
            for e_ in range(2):
                for i in range(4):
                    KcT, BKcT = KcT_rr.next()
                    V2c, BV2c = V2c_rr.next()
                    V2cv = V2c[:].rearrange("p (b c) -> p b c", c=256)

                    def mkload(KcT=KcT, BKcT=BKcT, V2cv=V2cv, BV2c=BV2c, i=i, p=p):
                        S.dma("sp", [(KcT[:], kcT_d[i, p, :, :])], D_kcT, BKcT)
                        S.dma("sp", [(V2cv, v2c_d[i, :, :, p * 256:(p + 1) * 256].rearrange("b s c -> s b c"))], D_v2c, BV2c)
                    units.append((KcT, BKcT, V2cv, BV2c, mkload))
            o_ps, Bo = acc_rr.next()
            d_ps, Bd = acc_rr.next()
            for e_ in range(2):
                h = 2 * p + e_
                block(p, e_, TM * 128, 128, KT[e_ * 64:(e_ + 1) * 64, TM * 128:(TM + 1) * 128], 128,
                      V2v[:, TM, e_ * 128:(e_ + 1) * 128], nFv[:, TM, h:h + 1], BnF,
                      maskn, Bmaskn, o_ps, Bo, d_ps, Bd, 0, e_ == 0, e_ == 1, BKT, BV2p,
                      pre=(units[0][4] if e_ == 0 else None))
            finish(p, TM * 128, 128, o_ps, Bo, d_ps, Bd)
            for j in range(FT // SQ):
                W = SQ * 128
                q0 = j * W
                mid = j * SQ + SQ // 2
                bq, Bbq = bias_rr.next()
                bqv = bq[:].rearrange("p (t h) -> p t h", h=16)

                def mkbias(bqv=bqv, mid=mid, Bbq=Bbq):
                    dve(lambda e: e.tensor_tensor(out=bqv, in0=nFv, in1=Frv[:, mid, :].unsqueeze(1).to_broadcast([128, NT, 16]),
                                                  op=ALU.subtract), [BnF, BFr], [Bbq])
                o_ps, Bo = acc_rr.next()
                d_ps, Bd = acc_rr.next()
                nblk = 2 * (1 + j * SQ + SQ)
                cnt = 0
                for e_ in range(2):
                    h = 2 * p + e_
                    block(p, e_, q0, W, KT[e_ * 64:(e_ + 1) * 64, TM * 128:TM * 128 + 16], 16,
                          V2v[0:16, TM, e_ * 128:(e_ + 1) * 128], bqv[0:16, TM, h:h + 1], Bbq,
                          None, None, o_ps, Bo, d_ps, Bd, 0, cnt == 0, cnt == nblk - 1, BKT, BV2p,
                          pre=(mkbias if e_ == 0 else None))
                    cnt += 1
                    for kt in range(j * SQ):
                        block(p, e_, q0, W, KT[e_ * 64:(e_ + 1) * 64, kt * 128:(kt + 1) * 128], 128,
                              V2v[:, kt, e_ * 128:(e_ + 1) * 128], bqv[:, kt, h:h + 1], Bbq,
                              None, None, o_ps, Bo, d_ps, Bd, 0, cnt == 0, cnt == nblk - 1, BKT, BV2p)
                        cnt += 1
                    for jj in range(SQ):
                        kt = j * SQ + jj
                        block(p, e_, q0 + jj * 128, W - jj * 128, KT[e_ * 64:(e_ + 1) * 64, kt * 128:(kt + 1) * 128], 128,
                              V2v[:, kt, e_ * 128:(e_ + 1) * 128], bqv[:, kt, h:h + 1], Bbq,
                              maskn, Bmaskn, o_ps, Bo, d_ps, Bd, jj * 128, cnt == 0, cnt == nblk - 1, BKT, BV2p)
                        cnt += 1
                finish(p, q0, W, o_ps, Bo, d_ps, Bd)
            o_ps, Bo = acc_rr.next()
            d_ps, Bd = acc_rr.next()
            nblk = 2 * (1 + 4 * PB)
            cnt = 0
            for e_ in range(2):
                h = 2 * p + e_
                block(p, e_, TS * 128, 128, KT[e_ * 64:(e_ + 1) * 64, TS * 128:(TS + 1) * 128], 128,
                      V2v[:, TS, e_ * 128:(e_ + 1) * 128], nFv[:, TS, h:h + 1], BnF,
                      masknBD, BmasknBD, o_ps, Bo, d_ps, Bd, 0, cnt == 0, cnt == nblk - 1, BKT, BV2p)
                cnt += 1
                for i in range(4):
                    u = e_ * 4 + i
                    KcT, BKcT, V2cv, BV2c, _ = units[u]
                    nxt = units[u + 1][4] if u + 1 < 8 else None
                    for b in range(PB):
                        block(p, e_, TS * 128 + 32 * i, 32, KcT[e_ * 64:(e_ + 1) * 64, b * 128:(b + 1) * 128], 128,
                              V2cv[:, b, e_ * 128:(e_ + 1) * 128], revv[:, i, b, h:h + 1], Brevb,
                              None, None, o_ps, Bo, d_ps, Bd, 32 * i, cnt == 0, cnt == nblk - 1, BKcT, BV2c,
                              pre=(nxt if b == 0 else None))
                        cnt += 1
            finish(p, TS * 128, 128, o_ps, Bo, d_ps, Bd)
            flush()

    def phase_C():
        load_w_down_half(w_down[0], 0)
        oTl_rr = sbs("oTl", [128, 1024], BF16, 3)
        def tileC(ti):
            oTl, BoTl, x, Bx = getC(ti)
            yield
            for half in range(2):
                p_, Bp_ = mm_group(oTl, BoTl, half * 512, 512)
                dve(lambda e, half=half, p_=p_, x=x: e.tensor_tensor(out=x[:, half * 512:(half + 1) * 512], in0=p_[:, 0:512],
                                                                    in1=x[:, half * 512:(half + 1) * 512], op=ALU.add),
                    [Bp_, Bx], [Bx])
                yield
            store_x(ti, x, Bx)
        def loadC(ti):
            oTl, BoTl = oTl_rr.next()
            S.dma("sp", [(oTl[:].rearrange("p (j t) -> p j t", j=8),
                          oT_d[:, :, ti * 128:(ti + 1) * 128].rearrange("j p t -> p j t"))], D_oT, BoTl)
            x, Bx = load_x(ti)
            return oTl, BoTl, x, Bx
        getC = make_prefetcher(range(NT), loadC)
        run_interleaved([tileC(ti) for ti in range(NT)], lag=2)

    if "C" in PH:
        load_w(fox_w_out, 1024)
    if "B" in PH:
        run_phase(phase_B)
    if "C" in PH:
        run_phase(phase_C)


    groups = [list(range(g * 4, min(FT, g * 4 + 4))) for g in range((FT + 3) // 4)] + [[TS, TM]]
    hTs_d = dscr("hTs", [len(groups), 128, 4096], BF16); D_hTs = S.buf("D_hTs", multi=True)

    def phase_FFN(l, half, final):
        load_w(w_up[l], 2048, col0=half * 2048, dcol0=0)
        if half == 1:
            load_w_down_half(w_down[l], half)
        if half == 0:
            load_grep(g_ffn[l, :])
        elif final:
            load_grep(g_final[0, :])
        xg_rr = sbs("xg", [128, 4096], F32, 2)
        hT4_rr = sbs("hT4", [128, 4096], BF16, 2)
        aT, BaT = sb("aT", [128, 16 * 512], BF16)
        rt_rr = sbs("rt", [128, 512], F32, 2)
        yo_rr = sbs("yo", [128, 1024], F32, 2) if final else None
        if os.environ.get('BURN') == '1':
            xg_rr.next(); hT4_rr.next()
        def loadF(gi):
            tiles = groups[gi]
            xg, Bxg = xg_rr.next()
            hT4, BhT4 = hT4_rr.next()
            S.dma("sp", [(xg[:, s_ * 1024:(s_ + 1) * 1024], xs_d[ti * 128:(ti + 1) * 128, :]) for s_, ti in enumerate(tiles)],
                  D_xs[tiles[0]], Bxg, extra_reads=[D_xs[t] for t in tiles[1:]])
            if half == 1:
                S.dma("sp", [(hT4[:], hTs_d[gi, :, :])], D_hTs, BhT4)
            return xg, Bxg, hT4, BhT4
        getF = make_prefetcher(range(len(groups)), loadF)
        order = list(enumerate(groups))
        if os.environ.get('GORD') == '1':
            order = order[::-1]
        def do_group(gi, tiles):
                n = len(tiles)
                GW = 128 * n
                xg, Bxg, hT4, BhT4 = getF(gi)
                hT4v = hT4[:].rearrange("p (k t) -> p k t", k=8)
                if half == 0:
                    for s_, ti in enumerate(tiles):
                        hb, Bhb = hbb_rr.next()
                        rmsnorm_tile(xg[:, s_ * 1024:(s_ + 1) * 1024], Bxg, hb[:], Bhb)
                        transpose_bf(hb, Bhb, hT4v[:, :, s_ * 128:(s_ + 1) * 128], BhT4)
                        if os.environ.get('DBGH') == '1':
                            dve(lambda e, hb=hb, s_=s_, xg=xg: e.tensor_copy(out=xg[:, s_ * 1024:(s_ + 1) * 1024], in_=hb[:]), [Bhb], [Bxg])
                    S.dma("sp", [(hTs_d[gi, :, :], hT4[:])], BhT4, D_hTs)
                for fc in range(16 if os.environ.get('DBGH') != '1' else 0):
                    pu, Bpu = psrr.next()
                    for kc in range(8):
                        pe(lambda e, kc=kc, fc=fc, pu=pu: e.matmul(pu[:, 0:GW], lhsT=WK[kc][0][:, fc * 128:(fc + 1) * 128],
                                                                   rhs=hT4v[:, kc, 0:GW], start=(kc == 0), stop=(kc == 7)),
                           [BhT4, WK[kc][1]], [Bpu])
                    rt, Brt = rt_rr.next()
                    if os.environ.get('FFNV', 'a') == 'a':
                        act(lambda e, pu=pu, rt=rt: e.activation(out=rt[:, 0:GW], in_=pu[:, 0:GW], func=AF.Relu), [Bpu], [Brt])
                        dve(lambda e, pu=pu, rt=rt, fc=fc: e.tensor_tensor(out=aT[:, fc * 512:fc * 512 + GW], in0=rt[:, 0:GW], in1=pu[:, 0:GW],
                                                                          op=ALU.mult), [Brt, Bpu], [BaT])
                    else:
                        dve(lambda e, pu=pu, rt=rt: e.tensor_scalar(out=rt[:, 0:GW], in0=pu[:, 0:GW], scalar1=0.0, scalar2=None, op0=ALU.max),
                            [Bpu], [Brt])
                        act(lambda e, rt=rt, fc=fc: e.activation(out=aT[:, fc * 512:fc * 512 + GW], in_=rt[:, 0:GW], func=AF.Square), [Brt], [BaT])
                for s_, ti in enumerate(tiles):
                    for dh in range(2 if os.environ.get('DBGH') != '1' else 0):
                        py, Bpy = psrr.next()
                        for fc in range(16):
                            pe(lambda e, fc=fc, py=py, s_=s_, dh=dh: e.matmul(
                                py[:, 0:512], lhsT=aT[:, fc * 512 + s_ * 128:fc * 512 + (s_ + 1) * 128],
                                rhs=WK[fc // 2][0][:, 2048 + (fc % 2) * 1024 + dh * 512:2048 + (fc % 2) * 1024 + (dh + 1) * 512],
                                start=(fc == 0), stop=(fc == 15)), [BaT, WK[fc // 2][2]], [Bpy])
                        c0 = s_ * 1024 + dh * 512
                        dve(lambda e, py=py, c0=c0, xg=xg: e.tensor_tensor(out=xg[:, c0:c0 + 512], in0=py[:, 0:512], in1=xg[:, c0:c0 + 512],
                                                                          op=ALU.add), [Bpy, Bxg], [Bxg])
                    S.dma("sp", [(xs_d[ti * 128:(ti + 1) * 128, :], xg[:, s_ * 1024:(s_ + 1) * 1024])], Bxg, D_xs[ti])
                    if final and half == 1:
                        yo, Byo = yo_rr.next()
                        rmsnorm_tile(xg[:, s_ * 1024:(s_ + 1) * 1024], Bxg, yo[:], Byo)
                        S.dma("sp", [(y_out[ti * 128:(ti + 1) * 128, :], yo[:])], Byo, D_out)


        for gi, tiles in order:
            do_group(gi, tiles)

    if "D" in PH:
        run_phase(phase_FFN, 0, 0, False)
        if os.environ.get("FFNH", "1") == "1":
            run_phase(phase_FFN, 0, 1, False)

    NCH = 1 + 2 * FT + 4
    mq_d = dscr("mq", [R, 512], F32); mk_d = dscr("mk", [R, 512], F32)
    mv_d = dscr("mv", [R, 1024], BF16); mo_d = dscr("mo", [R, 1024], F32)
    tok_d = dscr("tok", [R, 20], F32); hg_d = dscr("hg", [R, 1024], BF16)
    D_ml = S.buf("D_ml", multi=True)
    D_hg = S.buf("D_hg", multi=True)
    dec_row, Bdec = sb("dec_row", [4, NCH], F32)

    def phase_E():
        load_w(ml_w_in, 3080)
        load_grep(g_mix[1, :])
        bi_rep, Bbi = sb("bi_rep", [128, 4], F32)
        bfm_rep, Bbfm = sb("bfm_rep", [128, 4], F32)
        m0r, Bm0r = sb("m0r", [4, 4], F32)
        S.dma("sp", [(bi_rep[:], ml_b_i[0, :].partition_broadcast(128))], D_in, Bbi)
        S.dma("sp", [(bfm_rep[:], ml_b_f[0, :].partition_broadcast(128))], D_in, Bbfm)
        m0c, Bm0c = sb("m0c", [4, 4], F32)
        S.dma("sp", [(m0c[:], mm0[:, :])], D_in, Bm0c)
        pm, Bpm = psrr.next()
        pe(lambda e: e.transpose(out=pm[0:4, 0:4], in_=m0c[0:4, 0:4], identity=ident[0:4, 0:4]), [Bm0c, Bident], [Bpm])
        act(lambda e: e.activation(out=m0r[:], in_=pm[0:4, 0:4], func=AF.Copy), [Bpm], [Bm0r])
        car4, Bcar4 = sb("car4", [128, 4], F32)
        gcar, Bgcar = sb("gcar", [4, 1], F32)
        pool(lambda e: e.memset(car4[:], 0.0), [], [Bcar4])
        pool(lambda e: e.memset(gcar[:], 0.0), [], [Bgcar])
        qf_rr = sbs("qf", [128, 512], F32, 2)
        kf2_rr = sbs("kf2", [128, 512], F32, 2)
        vb_rr = sbs("vb", [128, 1024], BF16, 2)
        of_rr = sbs("of", [128, 1024], F32, 2)
        g8_rr = sbs("g8", [128, 16], F32, 2)
        row_rr = sbs("rowt", [4, 6 * 128], F32, 2)
        tokv_rr = sbs("tokv", [128, 24], F32, 2)

        def tileE(ti):
            x, Bx = getE(ti)
            hT, BhT = norm_and_transpose(x, Bx)
            yield
            rows = slice(ti * 128, (ti + 1) * 128)
            qf, Bqf = qf_rr.next()
            p, Bp = mm_group(hT, BhT, 0, 512)
            act(lambda e: e.activation(out=qf[:], in_=p[:, 0:512], func=AF.Copy), [Bp], [Bqf])
            S.dma("sp", [(mq_d[rows, :], qf[:])], Bqf, D_ml)
            yield
            kf2, Bkf2 = kf2_rr.next()
            p2, Bp2 = mm_group(hT, BhT, 512, 512)
            act(lambda e: e.activation(out=kf2[:], in_=p2[:, 0:512], func=AF.Copy, scale=float(128.0 ** -0.5)), [Bp2], [Bkf2])
            S.dma("sp", [(mk_d[rows, :], kf2[:])], Bkf2, D_ml)
            yield
            vb, Bvb = vb_rr.next()
            of, Bof = of_rr.next()
            for half in range(2):
                p3, Bp3 = mm_group(hT, BhT, 1024 + half * 512, 512)
                act(lambda e, half=half, p3=p3: e.activation(out=vb[:, half * 512:(half + 1) * 512], in_=p3[:, 0:512], func=AF.Copy),
                    [Bp3], [Bvb])
            yield
            for half in range(2):
                p4, Bp4 = mm_group(hT, BhT, 2048 + half * 512, 512)
                act(lambda e, half=half, p4=p4: e.activation(out=of[:, half * 512:(half + 1) * 512], in_=p4[:, 0:512], func=AF.Sigmoid),
                    [Bp4], [Bof])
            S.dma("sp", [(mv_d[rows, :], vb[:])], Bvb, D_ml)
            S.dma("sp", [(mo_d[rows, :], of[:])], Bof, D_ml)
            yield
            pg, Bpg = mm_group(hT, BhT, 3072, 8)
            g8, Bg8 = g8_rr.next()
            dve(lambda e: e.tensor_tensor(out=g8[:, 0:4], in0=pg[:, 0:4], in1=bi_rep[:], op=ALU.add), [Bpg, Bbi], [Bg8])
            dve(lambda e: e.tensor_tensor(out=g8[:, 4:8], in0=pg[:, 4:8], in1=bfm_rep[:], op=ALU.add), [Bpg, Bbfm], [Bg8])
            act(lambda e: e.activation(out=g8[:, 4:8], in_=g8[:, 4:8], func=AF.Exp, scale=-1.0), [Bg8], [Bg8])
            act(lambda e: e.activation(out=g8[:, 4:8], in_=g8[:, 4:8], func=AF.Ln, bias=1.0), [Bg8], [Bg8])
            dve(lambda e: e.tensor_scalar(out=g8[:, 4:8], in0=g8[:, 4:8], scalar1=-1.0, scalar2=None, op0=ALU.mult), [Bg8], [Bg8])
            if ti == TM:
                dve(lambda e: e.tensor_scalar(out=g8[:, 4:8], in0=g8[:, 4:8], scalar1=vmask[:, 0:1], scalar2=None, op0=ALU.mult),
                    [Bg8, Bvmask], [Bg8])
                dve(lambda e: e.tensor_scalar(out=g8[:, 0:4], in0=g8[:, 0:4], scalar1=vmask[:, 0:1], scalar2=vmask[:, 1:2],
                                              op0=ALU.mult, op1=ALU.add), [Bg8, Bvmask], [Bg8])
            pf, Bpf = psrr.next()
            if ti == TS:
                pe(lambda e: e.matmul(pf[:, 0:4], lhsT=triBD[:], rhs=g8[:, 4:8], start=True, stop=True), [BtriBD, Bg8], [Bpf])
                dve(lambda e: e.tensor_copy(out=g8[:, 8:12], in_=pf[:, 0:4]), [Bpf], [Bg8])
            else:
                pe(lambda e: e.matmul(pf[:, 0:4], lhsT=tri[:], rhs=g8[:, 4:8], start=True, stop=False), [Btri, Bg8], [Bpf])
                pe(lambda e: e.matmul(pf[:, 4:8], lhsT=ones[:], rhs=g8[:, 4:8], start=False, stop=True), [Bones, Bg8], [Bpf])
                dve(lambda e: e.tensor_tensor(out=g8[:, 8:12], in0=pf[:, 0:4], in1=car4[:], op=ALU.add), [Bpf, Bcar4], [Bg8])
                dve(lambda e: e.tensor_tensor(out=car4[:], in0=pf[:, 4:8], in1=car4[:], op=ALU.add), [Bpf, Bcar4], [Bcar4])
            dve(lambda e: e.tensor_tensor(out=g8[:, 12:16], in0=g8[:, 0:4], in1=g8[:, 8:12], op=ALU.subtract), [Bg8], [Bg8])
            yield
            rw, Brw = row_rr.next()
            A_r, G_r, nG_r = rw[:, 0:128], rw[:, 128:256], rw[:, 256:384]
            wg_r, rs_r, wi_r = rw[:, 384:512], rw[:, 512:640], rw[:, 640:768]
            pa, Bpa = psrr.next()
            pe(lambda e: e.transpose(out=pa[0:4, 0:128], in_=g8[:, 12:16], identity=ident[:]), [Bg8, Bident], [Bpa])
            act(lambda e: e.activation(out=A_r, in_=pa[0:4, 0:128], func=AF.Copy), [Bpa], [Brw])
            if ti == TS:
                segs = [(32 * i, 32 * i + 32, m0r[:, i:i + 1], Bm0r) for i in range(4)]
                for (c0, c1, init, Binit) in segs:
                    dve(lambda e, c0=c0, c1=c1, init=init: e.tensor_tensor_scan(out=G_r[:, c0:c1], data0=A_r[:, c0:c1], data1=A_r[:, c0:c1],
                                                                                 initial=init, op0=ALU.max, op1=ALU.max), [Brw, Binit], [Brw])
                chunks = [(32 * i, 32 * i + 32, m0r[:, i:i + 1], 1 + 2 * FT + i) for i in range(4)]
            else:
                dve(lambda e: e.tensor_tensor_scan(out=G_r, data0=A_r, data1=A_r, initial=gcar[:, 0:1], op0=ALU.max, op1=ALU.max),
                    [Brw, Bgcar], [Brw])
                cb = 0 if ti == TM else 1 + 2 * ti
                chunks = [(0, 64, gcar[:, 0:1], cb), (64, 128, G_r[:, 63:64], cb + 1 if ti != TM else None)]
            dve(lambda e: e.tensor_scalar(out=nG_r, in0=G_r, scalar1=-1.0, scalar2=None, op0=ALU.mult), [Brw], [Brw])
            yield
            for (c0, c1, gp, cidx) in chunks:
                act(lambda e, c0=c0, c1=c1: e.activation(out=wg_r[:, c0:c1], in_=A_r[:, c0:c1], func=AF.Exp, bias=nG_r[:, c1 - 1:c1]),
                    [Brw], [Brw])
                act(lambda e, c0=c0, c1=c1: e.activation(out=rs_r[:, c0:c1], in_=nG_r[:, c0:c1], func=AF.Exp, bias=G_r[:, c1 - 1:c1]),
                    [Brw], [Brw])
                act(lambda e, c0=c0, c1=c1, gp=gp: e.activation(out=wi_r[:, c0:c1], in_=nG_r[:, c0:c1], func=AF.Exp, bias=gp),
                    [Brw, Bgcar, Bm0r], [Brw])
                if cidx is not None:
                    act(lambda e, c1=c1, cidx=cidx: e.activation(out=dec_row[:, cidx:cidx + 1], in_=wi_r[:, c1 - 1:c1], func=AF.Copy),
                        [Brw], [Bdec])
            if ti != TS:
                act(lambda e: e.activation(out=gcar[:, 0:1], in_=G_r[:, 127:128], func=AF.Copy), [Brw], [Bgcar])
            yield
            pb, Bpb = psrr.next()
            for j, src in enumerate((wg_r, rs_r, wi_r, G_r)):
                pe(lambda e, j=j, src=src: e.transpose(out=pb[:, j * 4:(j + 1) * 4], in_=src, identity=ident[0:4, 0:4]),
                   [Brw, Bident], [Bpb])
            tokv, Btokv = tokv_rr.next()
            act(lambda e: e.activation(out=tokv[:, 0:16], in_=pb[:, 0:16], func=AF.Copy), [Bpb], [Btokv])
            dve(lambda e: e.tensor_tensor(out=tokv[:, 20:24], in0=tokv[:, 12:16], in1=g8[:, 8:12], op=ALU.add), [Btokv, Bg8], [Btokv])
            act(lambda e: e.activation(out=tokv[:, 16:20], in_=tokv[:, 20:24], func=AF.Exp, scale=-1.0), [Btokv], [Btokv])
            S.dma("sp", [(tok_d[rows, :], tokv[:, 0:20])], Btokv, D_ml)
            if ti == FT - 1:
                S.dma("sp", [(m_out[0:1, :], tokv[127:128, 20:24])], Btokv, D_out)
            if ti == TS:
                S.dma("sp", [(m_out[1 + i:2 + i, :], tokv[32 * i + 31:32 * i + 32, 20:24]) for i in range(4)], Btokv, D_out)

        getE = make_prefetcher(tile_order, load_x)
        run_interleaved([tileE(ti) for ti in tile_order], lag=5)

    if "E" in PH:
        run_phase(phase_E)

    def phase_F():
        gh_rep, Bgh = sb("gh_rep", [128, 1024], F32)
        S.dma("sp", [(gh_rep[:], ml_g_h[0, :].partition_broadcast(128))], D_in, Bgh)
        onesb, Bonesb = sb("onesb", [128, 1], BF16)
        pool(lambda e: e.memset(onesb[:], 1.0), [], [Bonesb])
        dd, Bdd = sb("dd", [4, NCH * 4], F32)
        decrep, Bdecrep = sb("decrep", [128, NCH * 4], F32)
        dve(lambda e: e.tensor_tensor(out=dd[:].rearrange("p (c h) -> p c h", h=4),
                                      in0=ident[0:4, 0:4].unsqueeze(1).to_broadcast([4, NCH, 4]),
                                      in1=dec_row[:, 0:NCH].unsqueeze(2).to_broadcast([4, NCH, 4]), op=ALU.mult),
            [Bident, Bdec], [Bdd])
        pdc, Bpdc = psrr.next()
        pe(lambda e: e.matmul(pdc[:, 0:NCH * 4], lhsT=ones[0:4, :], rhs=dd[:], start=True, stop=True), [Bones, Bdd], [Bpdc])
        act(lambda e: e.activation(out=decrep[:], in_=pdc[:, 0:NCH * 4], func=AF.Copy), [Bpdc], [Bdecrep])

        C, BC = sb("Cst", [128, 1024], F32)
        nst, Bnst = sb("nst", [128, 4], F32)
        Cd, BCd = sb("Cd", [128, 1024], F32)
        Cdb, BCdb = sb("Cdb", [128, 1024], BF16)
        nd, Bnd = sb("nd", [128, 4], F32)
        ndb, Bndb = sb("ndb", [128, 4], BF16)
        qc_rr = sbs("qc", [64, 512], F32, 3)
        kc_rr = sbs("kc", [64, 512], F32, 3)
        vc_rr = sbs("vc", [64, 1024], BF16, 3)
        oc_rr = sbs("oc", [64, 1024], F32, 4)
        tk_rr = sbs("tk", [64, 20], F32, 4)
        kwf_rr = sbs("kwf", [64, 512], F32, 2)
        kwb_rr = sbs("kwb", [64, 512], BF16, 2)
        QT_rr = sbs("QTm", [128, 256], BF16, 2)
        KWT_rr = sbs("KWTm", [128, 256], BF16, 2)
        Sm_rr = sbs("Sm", [64, 256], BF16, 2)
        d_rr = sbs("dsm", [64, 8], F32, 3)
        hh_rr = sbs("hh", [64, 1024], F32, 2)
        hgb_rr = sbs("hgb", [64, 1024], BF16, 2)
        cio, Bcio = sb("cio", [128, 1024], F32)
        nio, Bnio = sb("nio", [4, 128], F32)
        Nsets = [[PS[0], PS[1]], [PS[2], PS[3]]]
        U = [PS[4], PS[5]]
        m_rr = RR(PS[6:8])

        def init_zero():
            pool(lambda e: e.memset(C[:], 0.0), [], [BC])
            pool(lambda e: e.memset(nst[:], 0.0), [], [Bnst])

        def init_from(i):
            S.dma("sp", [(cio[:].rearrange("p (h c k) -> p h c k", h=4, c=2),
                          mC0[i].rearrange("h (c p) k -> p h c k", p=128))], D_in, Bcio)
            for g in range(2):
                pt, Bpt = m_rr.next()
                for j in range(4):
                    blk = g * 4 + j
                    pe(lambda e, j=j, blk=blk, pt=pt: e.transpose(out=pt[:, j * 128:(j + 1) * 128], in_=cio[:, blk * 128:(blk + 1) * 128],
                                                                  identity=ident[:]), [Bcio, Bident], [Bpt])
                act(lambda e, g=g, pt=pt: e.activation(out=C[:, g * 512:(g + 1) * 512], in_=pt[:, 0:512], func=AF.Copy), [Bpt], [BC])
            S.dma("sp", [(nio[:], mn0[i, :, :])], D_in, Bnio)
            pn, Bpn = m_rr.next()
            pe(lambda e: e.transpose(out=pn[:, 0:4], in_=nio[0:4, :], identity=ident[0:4, 0:4]), [Bnio, Bident], [Bpn])
            act(lambda e: e.activation(out=nst[:], in_=pn[:, 0:4], func=AF.Copy), [Bpn], [Bnst])

        def write_state(idx):
            for g in range(2):
                pt, Bpt = m_rr.next()
                for j in range(4):
                    blk = g * 4 + j
                    pe(lambda e, j=j, blk=blk, pt=pt: e.transpose(out=pt[:, j * 128:(j + 1) * 128], in_=C[:, blk * 128:(blk + 1) * 128],
                                                                  identity=ident[:]), [BC, Bident], [Bpt])
                act(lambda e, g=g, pt=pt: e.activation(out=cio[:, g * 512:(g + 1) * 512], in_=pt[:, 0:512], func=AF.Copy), [Bpt], [Bcio])
            S.dma("sp", [(C_out[idx].rearrange("h (c p) k -> p h c k", p=128),
                          cio[:].rearrange("p (h c k) -> p h c k", h=4, c=2))], Bcio, D_out)
            pn, Bpn = m_rr.next()
            pe(lambda e: e.transpose(out=pn[0:4, 0:128], in_=nst[:, 0:4], identity=ident[:]), [Bnst, Bident], [Bpn])
            act(lambda e: e.activation(out=nio[:], in_=pn[0:4, 0:128], func=AF.Copy), [Bpn], [Bnio])
            S.dma("sp", [(n_out[idx, :, :], nio[:])], Bnio, D_out)

        def chunkF(r0, L, c):
            st = {}
            N = Nsets[c % 2]

            def stage0():
                qc, Bqc = qc_rr.next(); kc, Bkc = kc_rr.next(); vc, Bvc = vc_rr.next(); oc, Boc = oc_rr.next(); tk, Btk = tk_rr.next()
                S.dma("sp", [(qc[0:L, :], mq_d[r0:r0 + L, :])], D_ml, Bqc)
                S.dma("sp", [(kc[0:L, :], mk_d[r0:r0 + L, :])], D_ml, Bkc)
                S.dma("sp", [(vc[0:L, :], mv_d[r0:r0 + L, :])], D_ml, Bvc)
                S.dma("sp", [(oc[0:L, :], mo_d[r0:r0 + L, :])], D_ml, Boc)
                S.dma("sp", [(tk[0:L, :], tok_d[r0:r0 + L, :])], D_ml, Btk)
                st.update(locals())

            def stage1():
                qc, Bqc, kc, Bkc, vc, Bvc, oc, Boc, tk, Btk = [st[k] for k in ('qc','Bqc','kc','Bkc','vc','Bvc','oc','Boc','tk','Btk')]
                kwf, Bkwf = kwf_rr.next(); kwb, Bkwb = kwb_rr.next()
                dve(lambda e: e.tensor_tensor(out=kwf[0:L, :].rearrange("p (h d) -> p h d", h=4),
                                              in0=kc[0:L, :].rearrange("p (h d) -> p h d", h=4),
                                              in1=tk[0:L, 0:4].unsqueeze(2).to_broadcast([L, 4, 128]), op=ALU.mult), [Bkc, Btk], [Bkwf])
                act(lambda e: e.activation(out=kwb[0:L, :], in_=kwf[0:L, :], func=AF.Copy), [Bkwf], [Bkwb])
                QT, BQT = QT_rr.next(); KWT, BKWT = KWT_rr.next()
                for (src, Bsrc, dst, Bdst) in ((qc, Bqc, QT, BQT), (kwf, Bkwf, KWT, BKWT)):
                    pt, Bpt = m_rr.next()
                    for h in range(4):
                        pe(lambda e, h=h, src=src, pt=pt: e.transpose(out=pt[:, h * L:(h + 1) * L], in_=src[0:L, h * 128:(h + 1) * 128],
                                                                      identity=ident[0:L, 0:L]), [Bsrc, Bident], [Bpt])
                    act(lambda e, dst=dst, pt=pt: e.activation(out=dst[:, 0:4 * L], in_=pt[:, 0:4 * L], func=AF.Copy), [Bpt], [Bdst])
                pS, BpS = m_rr.next()
                for h in range(4):
                    pe(lambda e, h=h: e.matmul(pS[0:L, h * L:(h + 1) * L], lhsT=KWT[:, h * L:(h + 1) * L], rhs=QT[:, h * L:(h + 1) * L],
                                               start=True, stop=True), [BKWT, BQT], [BpS])
                Sm, BSm = Sm_rr.next()
                dve(lambda e: e.tensor_tensor(out=Sm[0:L, 0:4 * L].rearrange("p (h t) -> p h t", h=4),
                                              in0=pS[0:L, 0:4 * L].rearrange("p (h t) -> p h t", h=4),
                                              in1=tri[0:L, 0:L].unsqueeze(1).to_broadcast([L, 4, L]), op=ALU.mult), [BpS, Btri], [BSm])
                st.update(locals())

            def stage2():
                qc, Bqc, kc, Bkc, vc, Bvc, oc, Boc, tk, Btk = [st[k] for k in ('qc','Bqc','kc','Bkc','vc','Bvc','oc','Boc','tk','Btk')]
                kwb, Bkwb, QT, BQT, Sm, BSm = [st[k] for k in ('kwb','Bkwb','QT','BQT','Sm','BSm')]
                dve(lambda e: e.tensor_tensor(out=Cd[:].rearrange("p (h v) -> p h v", h=4), in0=C[:].rearrange("p (h v) -> p h v", h=4),
                                              in1=decrep[:, c * 4:(c + 1) * 4].unsqueeze(2).to_broadcast([128, 4, 256]), op=ALU.mult),
                    [BC, Bdecrep], [BCd])
                act(lambda e: e.activation(out=Cdb[:], in_=Cd[:], func=AF.Copy), [BCd], [BCdb])
                dve(lambda e: e.tensor_tensor(out=nd[:], in0=nst[:], in1=decrep[:, c * 4:(c + 1) * 4], op=ALU.mult), [Bnst, Bdecrep], [Bnd])
                act(lambda e: e.activation(out=ndb[:], in_=nd[:], func=AF.Copy), [Bnd], [Bndb])
                pD, BpD = m_rr.next()
                for h in range(4):
                    n_ps, Bn = N[h // 2]
                    co = (h % 2) * 256
                    pe(lambda e, h=h, n_ps=n_ps, co=co: e.matmul(n_ps[0:L, co:co + 256], lhsT=QT[:, h * L:(h + 1) * L],
                                                                 rhs=Cdb[:, h * 256:(h + 1) * 256], start=True, stop=False), [BQT, BCdb], [Bn])
                    pe(lambda e, h=h, n_ps=n_ps, co=co: e.matmul(n_ps[0:L, co:co + 256], lhsT=Sm[0:L, h * L:(h + 1) * L],
                                                                 rhs=vc[0:L, h * 256:(h + 1) * 256], start=False, stop=True), [BSm, Bvc], [Bn])
                    pe(lambda e, h=h: e.matmul(pD[0:L, h:h + 1], lhsT=QT[:, h * L:(h + 1) * L], rhs=ndb[:, h:h + 1], start=True, stop=False),
                       [BQT, Bndb], [BpD])
                    pe(lambda e, h=h: e.matmul(pD[0:L, h:h + 1], lhsT=Sm[0:L, h * L:(h + 1) * L], rhs=onesb[0:L, 0:1], start=False, stop=True),
                       [BSm, Bonesb], [BpD])
                pD2, BpD2 = m_rr.next()
                for h in range(4):
                    u_ps, Bu = U[h // 2]
                    co = (h % 2) * 256
                    pe(lambda e, h=h, u_ps=u_ps, co=co: e.matmul(u_ps[:, co:co + 256], lhsT=kwb[0:L, h * 128:(h + 1) * 128],
                                                                 rhs=vc[0:L, h * 256:(h + 1) * 256], start=True, stop=True), [Bkwb, Bvc], [Bu])
                    pe(lambda e, h=h: e.matmul(pD2[:, h:h + 1], lhsT=kwb[0:L, h * 128:(h + 1) * 128], rhs=onesb[0:L, 0:1], start=True, stop=True),
                       [Bkwb, Bonesb], [BpD2])
                for g in range(2):
                    u_ps, Bu = U[g]
                    dve(lambda e, g=g, u_ps=u_ps: e.tensor_tensor(out=C[:, g * 512:(g + 1) * 512], in0=u_ps[:, 0:512], in1=Cd[:, g * 512:(g + 1) * 512],
                                                                  op=ALU.add), [Bu, BCd], [BC])
                dve(lambda e: e.tensor_tensor(out=nst[:], in0=pD2[:, 0:4], in1=nd[:], op=ALU.add), [BpD2, Bnd], [Bnst])
                ds, Bds = d_rr.next()
                dve(lambda e: e.tensor_tensor(out=ds[0:L, 0:4], in0=pD[0:L, 0:4], in1=tk[0:L, 4:8], op=ALU.mult), [BpD, Btk], [Bds])
                st["ds"] = (ds, Bds)

            def stage2b():
                oc, Boc, tk, Btk = [st[k] for k in ('oc', 'Boc', 'tk', 'Btk')]
                ds, Bds = st["ds"]
                act(lambda e: e.activation(out=ds[0:L, 0:4], in_=ds[0:L, 0:4], func=AF.Abs), [Bds], [Bds])
                dve(lambda e: e.tensor_tensor(out=ds[0:L, 0:4], in0=ds[0:L, 0:4], in1=tk[0:L, 16:20], op=ALU.max), [Bds, Btk], [Bds])
                dve(lambda e: e.reciprocal(out=ds[0:L, 0:4], in_=ds[0:L, 0:4]), [Bds], [Bds])
                dve(lambda e: e.tensor_tensor(out=ds[0:L, 0:4], in0=ds[0:L, 0:4], in1=tk[0:L, 4:8], op=ALU.mult), [Bds, Btk], [Bds])
                hh, Bhh = hh_rr.next()
                for h in range(4):
                    n_ps, Bn = N[h // 2]
                    co = (h % 2) * 256
                    act(lambda e, h=h, n_ps=n_ps, co=co: e.activation(out=hh[0:L, h * 256:(h + 1) * 256], in_=n_ps[0:L, co:co + 256], func=AF.Copy,
                                                                      scale=ds[0:L, h:h + 1]), [Bn, Bds], [Bhh])
                for h in range(4):
                    act(lambda e, h=h: e.activation(out=junk[0:L, 0:256], in_=hh[0:L, h * 256:(h + 1) * 256], func=AF.Square,
                                                    accum_out=ds[0:L, 4 + h:5 + h]), [Bhh], [Bjunk, Bds])
                rstd_from_ss(ds[0:L, 4:8], Bds, 256.0)
                dve(lambda e: e.tensor_tensor(out=hh[0:L, :].rearrange("p (h v) -> p h v", h=4), in0=hh[0:L, :].rearrange("p (h v) -> p h v", h=4),
                                              in1=ds[0:L, 4:8].unsqueeze(2).to_broadcast([L, 4, 256]), op=ALU.mult), [Bhh, Bds], [Bhh])
                pool(lambda e: e.tensor_tensor(out=hh[0:L, :], in0=hh[0:L, :], in1=gh_rep[0:L, :], op=ALU.mult), [Bhh, Bgh], [Bhh])
                hgb, Bhgb = hgb_rr.next()
                pool(lambda e: e.tensor_tensor(out=hgb[0:L, :], in0=hh[0:L, :], in1=oc[0:L, :], op=ALU.mult), [Bhh, Boc], [Bhgb])
                S.dma("sp", [(hg_d[r0:r0 + L, :], hgb[0:L, :])], Bhgb, D_hg)

            return stage0, stage1, stage2, stage2b

        sched = []

        def pre0():
            init_zero()
            pool(lambda e: e.memset(Cdb[:], 0.0), [], [BCdb])
            S.dma("sp", [(hg_d[TM * 128 + 64:(TM + 1) * 128, :], Cdb[0:64, :])], BCdb, D_hg)
        sched.append(chunkF(TM * 128, 64, 0) + (pre0, None))
        for t in range(FT):
            for j in range(2):
                last = (t == FT - 1 and j == 1)
                sched.append(chunkF(t * 128 + 64 * j, 64, 1 + 2 * t + j) + (None, (lambda: write_state(0)) if last else None))
        for i in range(4):
            sched.append(chunkF(TS * 128 + 32 * i, 32, 1 + 2 * FT + i) + ((lambda i=i: init_from(i)), (lambda i=i: write_state(1 + i))))
        n = len(sched)
        for k in range(n + 3):
            if k < n:
                sched[k][0]()
            if 1 <= k <= n:
                sched[k - 1][1]()
            if 2 <= k <= n + 1:
                s0_, s1_, s2_, s3_, pre_, post_ = sched[k - 2]
                if pre_ is not None:
                    pre_()
                s2_()
                if post_ is not None:
                    post_()
            if k >= 3:
                sched[k - 3][3]()

    def phase_G():
        load_w_down_half(w_down[1], 0)
        hgl_rr = sbs("hgl", [128, 1024], BF16, 3)

        def tileG(ti):
            hgl, Bhgl, x, Bx = getG(ti)
            yield
            hT, BhT = hT_rr.next()
            transpose_bf(hgl, Bhgl, hT[:], BhT)
            yield
            for half in range(2):
                p_, Bp_ = mm_group(hT, BhT, half * 512, 512)
                dve(lambda e, half=half, p_=p_: e.tensor_tensor(out=x[:, half * 512:(half + 1) * 512], in0=p_[:, 0:512],
                                                               in1=x[:, half * 512:(half + 1) * 512], op=ALU.add), [Bp_, Bx], [Bx])
                yield
            store_x(ti, x, Bx)
        def loadG(ti):
            hgl, Bhgl = hgl_rr.next()
            S.dma("sp", [(hgl[:], hg_d[ti * 128:(ti + 1) * 128, :])], D_hg, Bhgl)
            x, Bx = load_x(ti)
            return hgl, Bhgl, x, Bx
        getG = make_prefetcher(range(NT), loadG)
        run_interleaved([tileG(ti) for ti in range(NT)], lag=2)

    if "G" in PH:
        load_w(ml_w_out, 1024)
    if "F" in PH:
        run_phase(phase_F)
    if "G" in PH:
        run_phase(phase_G)
    if "H" in PH:
        run_phase(phase_FFN, 1, 0, False)
        run_phase(phase_FFN, 1, 1, True)

    if DEBUG:
        dbg = dout("dbg_x", [R, 1024])
        dbt_rr = sbs("dbt", [128, 1024], F32, 2)
        for ti in range(NT):
            t_, B_ = dbt_rr.next()
            S.dma("sp", [(t_[:], xs_d[ti * 128:(ti + 1) * 128, :])], D_xs[ti], B_)
            S.dma("sp", [(dbg[ti * 128:(ti + 1) * 128, :], t_[:])], B_, D_out)
    if DEBUG and "D" in PH and os.environ.get("DBH") == "1":
        dbh = dout("dbg_h", [len(groups), 128, 4096], BF16)
        dh_rr = sbs("dbh", [128, 4096], BF16, 2)
        for gi in range(len(groups)):
            t_, B_ = dh_rr.next()
            S.dma("sp", [(t_[:], hTs_d[gi, :, :])], D_hTs, B_)
            S.dma("sp", [(dbh[gi, :, :], t_[:])], B_, D_out)
    stats = S.emit()
    es.close()
    return nc, stats


_CACHE = {}


def kernel(x_prompt, x_sample, cache_fox_k, cache_fox_v, cache_fox_logf,
           state_mlstm_C, state_mlstm_n, state_mlstm_m, meta_tokens,
           g_mix, g_ffn, fox_w_in, fox_b_f, fox_g_q, fox_g_k, fox_w_out,
           mlstm_w_in, mlstm_b_i, mlstm_b_f, mlstm_g_h, mlstm_w_out,
           ffn_w_up, ffn_w_down, g_final):
    f = lambda a: np.ascontiguousarray(np.asarray(a, dtype=np.float32))
    x_prompt, x_sample = f(x_prompt), f(x_sample)
    NB, SEQ = x_prompt.shape[0], x_prompt.shape[1]
    FT = SEQ // 128
    PAST = cache_fox_k.shape[2]
    PB = PAST // 128
    TS, TM = FT, FT + 1
    key = (FT, PB)
    if key not in _CACHE:
        _CACHE[key] = build_program(FT, PB)
    nc, stats = _CACHE[key]
    in_maps = []
    zpad = np.zeros((112, 1024), np.float32)
    shared = {
        "g_mix": f(g_mix), "g_ffn": f(g_ffn), "g_final": f(g_final).reshape(1, 1024),
        "fox_w_in": f(fox_w_in)[0], "fox_b_f": f(fox_b_f), "fox_g_q": f(fox_g_q), "fox_g_k": f(fox_g_k),
        "fox_w_out": f(fox_w_out)[0], "ml_w_in": f(mlstm_w_in)[0], "ml_b_i": f(mlstm_b_i), "ml_b_f": f(mlstm_b_f),
        "ml_g_h": f(mlstm_g_h).reshape(1, 1024), "ml_w_out": f(mlstm_w_out)[0],
        "w_up": f(ffn_w_up), "w_down": f(ffn_w_down),
    }
    ckk, cvv, cll = f(cache_fox_k)[0], f(cache_fox_v)[0], f(cache_fox_logf)[0]
    sC, sn, sm = f(state_mlstm_C)[0], f(state_mlstm_n)[0], f(state_mlstm_m)[0]
    meta = f(meta_tokens)
    for c in range(NB):
        xin = np.concatenate([x_prompt[c], x_sample[4 * c:4 * c + 4].reshape(128, 1024), meta, zpad], axis=0)
        m = dict(shared)
        m["xin"] = xin
        m["ck"] = ckk[4 * c:4 * c + 4].reshape(4, PAST, 1024)
        m["cv"] = cvv[4 * c:4 * c + 4].reshape(4, PAST, 1024)
        m["cl"] = cll[4 * c:4 * c + 4]
        m["mC0"] = sC[4 * c:4 * c + 4]; m["mn0"] = sn[4 * c:4 * c + 4]; m["mm0"] = sm[4 * c:4 * c + 4]
        in_maps.append(m)
    res = run_bass_kernel_spmd(nc, in_maps, core_ids=list(range(NB)))
    rs = res.results
    _CACHE["last"] = rs

    def rows(name, w):
        a = np.stack([r[name] for r in rs])
        prompt = np.concatenate([a[:, TM * 128:TM * 128 + 16], a[:, 0:SEQ]], axis=1)
        samp = a[:, TS * 128:(TS + 1) * 128].reshape(4 * NB, 32, w)
        return prompt, samp
    yp, ys = rows("y_out", 1024)
    kp, ks = rows("k_out", 1024)
    vp, vs = rows("v_out", 1024)
    lp, ls = rows("lf_out", 16)
    Co = np.stack([r["C_out"] for r in rs]); no = np.stack([r["n_out"] for r in rs]); mo = np.stack([r["m_out"] for r in rs])
    L = SEQ + 16
    return (np.ascontiguousarray(yp[:, 16:]), ys,
            kp.reshape(1, NB, L, 16, 64), vp.reshape(1, NB, L, 16, 64), lp.reshape(1, NB, L, 16),
            np.ascontiguousarray(Co[:, 0])[None], np.ascontiguousarray(no[:, 0])[None], np.ascontiguousarray(mo[:, 0])[None],
            ks.reshape(1, 4 * NB, 32, 16, 64), vs.reshape(1, 4 * NB, 32, 16, 64), ls.reshape(1, 4 * NB, 32, 16),
            Co[:, 1:].reshape(1, 4 * NB, 4, 256, 128), no[:, 1:].reshape(1, 4 * NB, 4, 128), mo[:, 1:].reshape(1, 4 * NB, 4))
```

```python
import os
import contextlib
import numpy as np
import concourse.bass as bass
import concourse.mybir as mybir
from concourse.bass_utils import run_bass_kernel_spmd

F32 = mybir.dt.float32
BF16 = mybir.dt.bfloat16
AF = mybir.ActivationFunctionType
ALU = mybir.AluOpType
AX = mybir.AxisListType

ENGS = ("pe", "act", "dve", "pool", "sp")
NT = 34
R = NT * 128
T_SAMP = 32
T_META = 33
EPS = 1e-6


class Buf:
    __slots__ = ("name", "writers", "readers", "multi", "sem", "semval", "st_sem", "st_semval", "kind", "st_kind")

    def __init__(self, name, multi=False):
        self.name = name
        self.writers = []
        self.readers = []
        self.multi = multi
        self.sem = None
        self.semval = 0
        self.st_sem = None
        self.st_semval = 0
        self.kind = 'hw'
        self.st_kind = 'sw'


class Op:
    __slots__ = ("eng", "fn", "deps", "pos", "is_dma", "sem", "semval", "signal", "sigval")

    def __init__(self, eng, fn):
        self.eng = eng
        self.fn = fn
        self.deps = []
        self.pos = -1
        self.is_dma = False
        self.sem = None
        self.semval = 0
        self.signal = False
        self.sigval = 0


class Sched:
    def __init__(self, nc, same_engine_sync=True):
        self.nc = nc
        self.ops = {e: [] for e in ENGS}
        self.same_engine_sync = same_engine_sync
        self._sem_ctx = []
        self.bufs = []
        self.nsem = 0
        self.pre = []
        self.last_dma = {}
        self.sem_pool = {'hw': [], 'sw': []}
        self.store_eng = None

    def new_sem(self, name):
        cm = self.nc.semaphore(name)
        h = cm.__enter__()
        self._sem_ctx.append(cm)
        self.nsem += 1
        return h

    def buf(self, name, multi=False):
        b = Buf(name, multi)
        b.readers = list(self.pre)
        self.bufs.append(b)
        return b

    def release_sems(self, bufs):
        for b in bufs:
            if b.sem is not None:
                self.sem_pool[b.kind].append((b.sem, b.semval))
            if b.st_sem is not None:
                self.sem_pool[b.st_kind].append((b.st_sem, b.st_semval))

    def phase_mark(self):
        self.pre = [self.ops[e][-1] for e in ('pe', 'act', 'dve', 'pool') if self.ops[e]] + list(self.last_dma.values())

    def _track(self, op, reads, writes):
        deps = []
        for b in reads:
            deps.extend(b.writers)
        for b in writes:
            if not b.multi:
                deps.extend(b.readers)
                deps.extend(b.writers)
        seen = set()
        for d in deps:
            if d is op or id(d) in seen:
                continue
            seen.add(id(d))
            op.deps.append(d)
        for b in reads:
            if not op.is_dma:
                b.readers = [r for r in b.readers if r.is_dma or r.eng != op.eng]
            b.readers.append(op)
        for b in writes:
            if b.multi:
                b.writers.append(op)
            else:
                b.writers = [op]
                b.readers = []

    def op(self, eng, fn, reads=(), writes=()):
        o = Op(eng, fn)
        o.pos = len(self.ops[eng])
        self.ops[eng].append(o)
        self._track(o, reads, writes)
        return o

    def dma(self, eng, pairs, src, dst, extra_reads=(), **kw):
        if self.store_eng is not None and (dst.multi or dst.name.startswith('D_')):
            eng = self.store_eng
        o = Op(eng, None)
        o.is_dma = True
        if not dst.multi:
            if dst.sem is None:
                kind = 'sw' if eng == 'pool' else 'hw'
                dst.kind = kind
                if self.sem_pool[kind]:
                    dst.sem, dst.semval = self.sem_pool[kind].pop()
                else:
                    dst.sem = self.new_sem("d_" + dst.name)
            dst.semval += 16 * len(pairs)
            o.sem, o.semval = dst.sem, dst.semval
        else:
            if src.st_sem is None:
                kind = 'sw' if eng == 'pool' else 'hw'
                src.st_kind = kind
                if self.sem_pool[kind]:
                    src.st_sem, src.st_semval = self.sem_pool[kind].pop()
                else:
                    src.st_sem = self.new_sem("s_" + src.name)
            src.st_semval += 16 * len(pairs)
            o.sem, o.semval = src.st_sem, src.st_semval
        sem = o.sem
        self.last_dma[id(sem)] = o

        def fn(e, pairs=pairs, sem=sem, kw=kw):
            for (out_ap, in_ap) in pairs:
                e.dma_start(out=out_ap, in_=in_ap, **kw).then_inc(sem, 16)
        o.fn = fn
        o.pos = len(self.ops[eng])
        self.ops[eng].append(o)
        self._track(o, [src] + list(extra_reads), [dst])
        return o

    def emit(self):
        nc = self.nc
        for e in ENGS:
            for o in self.ops[e]:
                for d in o.deps:
                    if not d.is_dma:
                        if d.eng == o.eng and (d.eng == "pe" or not self.same_engine_sync):
                            continue
                        d.signal = True
        esem = {}
        for e in ENGS:
            if any(o.signal for o in self.ops[e]):
                esem[e] = self.new_sem("e_" + e)
            c = 0
            for o in self.ops[e]:
                if o.signal and not o.is_dma:
                    c += 1
                    o.sigval = c
        final = {}
        for b in self.bufs:
            if b.st_sem is not None:
                final[b.st_sem] = max(final.get(b.st_sem, 0), b.st_semval)
            if b.sem is not None:
                final[b.sem] = max(final.get(b.sem, 0), b.semval)
        handles = {"pe": "tensor", "act": "scalar", "dve": "vector", "pool": "gpsimd", "sp": "sync"}
        stats = {e: [len(self.ops[e]), 0] for e in ENGS}
        same = self.same_engine_sync

        def run_engine(ename, eh):
            waited = {}
            for o in self.ops[ename]:
                for d in o.deps:
                    if d.is_dma:
                        s, v = d.sem, d.semval
                    else:
                        if d.eng == ename and (ename == "pe" or not same):
                            continue
                        s, v = esem[d.eng], d.sigval
                    if waited.get(s, 0) >= v:
                        continue
                    waited[s] = v
                    eh.wait_ge(s, v)
                    stats[ename][1] += 1
                if o.is_dma:
                    o.fn(eh)
                else:
                    ins = o.fn(eh)
                    if o.signal:
                        ins.then_inc(esem[ename], 1)
            if ename == "sp":
                for s, v in final.items():
                    if waited.get(s, 0) < v:
                        eh.wait_ge(s, v)

        with nc.Block() as block:
            for ename in ENGS:
                deco = getattr(block, handles[ename])

                def body(eh, ename=ename):
                    run_engine(ename, eh)
                deco(body)
        for cm in reversed(self._sem_ctx):
            cm.__exit__(None, None, None)
        return stats


class RR:
    def __init__(self, items):
        self.items = items
        self.i = 0

    def next(self):
        it = self.items[self.i % len(self.items)]
        self.i += 1
        return it


def build_program(FT=32, PB=16):
    NT = FT + 2
    R = NT * 128
    TS = FT
    TM = FT + 1
    PH = os.environ.get("PH", "ABCDEFGH")
    DEBUG = os.environ.get("KDEBUG", "") == "1"
    nc = bass.Bass("TRN2", target_bir_lowering=False)
    es = contextlib.ExitStack()
    S = Sched(nc, same_engine_sync=(os.environ.get('SES', '1') == '1'))
    S.store_eng = os.environ.get('STQ', 'pool') or None

    def din(name, shape, dt=F32):
        return nc.dram_tensor(name, list(shape), dt, kind="ExternalInput").ap()

    def dout(name, shape, dt=F32):
        return nc.dram_tensor(name, list(shape), dt, kind="ExternalOutput").ap()

    def dscr(name, shape, dt):
        return nc.dram_tensor(name, list(shape), dt, kind="Internal").ap()

    cur = [es]

    uid = [0]

    def sb(name, shape, dt):
        uid[0] += 1
        name = "%s_%d" % (name, uid[0])
        t = cur[0].enter_context(nc.sbuf_tensor(name, list(shape), dt))
        return t, S.buf(name)

    def run_phase(fn, *a):
        if os.environ.get('NOLOCAL') == '1' and fn.__name__ == 'phase_FFN':
            fn(*a)
            return
        n0 = len(S.bufs)
        with contextlib.ExitStack() as pes:
            cur[0] = pes
            fn(*a)
            cur[0] = es
        S.phase_mark()
        S.release_sems(S.bufs[n0:])

    def sbs(name, shape, dt, n):
        return RR([sb("%s%d" % (name, i), shape, dt) for i in range(n)])

    def pool(fn, reads, writes):
        return S.op("pool", fn, reads, writes)

    def dve(fn, reads, writes):
        return S.op("dve", fn, reads, writes)

    def act(fn, reads, writes):
        return S.op("act", fn, reads, writes)

    def pe(fn, reads, writes):
        return S.op("pe", fn, reads, writes)

    xin = din("xin", [R, 1024])
    ck = din("ck", [4, PB * 128, 1024]); cv = din("cv", [4, PB * 128, 1024]); cl = din("cl", [4, PB * 128, 16])
    mC0 = din("mC0", [4, 4, 256, 128]); mn0 = din("mn0", [4, 4, 128]); mm0 = din("mm0", [4, 4])
    g_mix = din("g_mix", [2, 1024]); g_ffn = din("g_ffn", [2, 1024]); g_final = din("g_final", [1, 1024])
    fox_w_in = din("fox_w_in", [1024, 3088]); fox_b_f = din("fox_b_f", [1, 16])
    fox_g_q = din("fox_g_q", [1, 64]); fox_g_k = din("fox_g_k", [1, 64])
    fox_w_out = din("fox_w_out", [1024, 1024])
    ml_w_in = din("ml_w_in", [1024, 3080]); ml_b_i = din("ml_b_i", [1, 4]); ml_b_f = din("ml_b_f", [1, 4])
    ml_g_h = din("ml_g_h", [1, 1024]); ml_w_out = din("ml_w_out", [1024, 1024])
    w_up = din("w_up", [2, 1024, 4096]); w_down = din("w_down", [2, 4096, 1024])

    y_out = dout("y_out", [R, 1024])
    k_out = dout("k_out", [R, 1024]); v_out = dout("v_out", [R, 1024]); lf_out = dout("lf_out", [R, 16])
    C_out = dout("C_out", [5, 4, 256, 128]); n_out = dout("n_out", [5, 4, 128]); m_out = dout("m_out", [5, 4])

    D_in = S.buf("D_in", multi=True)
    D_out = S.buf("D_out", multi=True)

    xs_d = dscr("xs", [R, 1024], F32)
    D_xs = [S.buf("D_xs%d" % i) for i in range(NT)]
    kT_d = dscr("kT", [8, 128, R], BF16); D_kT = S.buf("D_kT", multi=True)
    qT_d = dscr("qT", [8, 128, R], BF16); D_qT = S.buf("D_qT", multi=True)
    v2_d = dscr("v2", [NT, 128, 2048], BF16); D_v2 = S.buf("D_v2", multi=True)
    oT_d = dscr("oT", [8, 128, R], BF16); D_oT = S.buf("D_oT", multi=True)
    kcT_d = dscr("kcT", [4, 8, 128, PB * 128], BF16); D_kcT = S.buf("D_kcT", multi=True)
    v2c_d = dscr("v2c", [4, PB, 128, 2048], BF16); D_v2c = S.buf("D_v2c", multi=True)

    PS = []
    for i in range(8):
        t = es.enter_context(nc.psum_tensor("ps%d" % i, [128, 512], F32))
        PS.append((t, S.buf("ps%d" % i)))
    psrr = RR(PS)

    ident, Bident = sb("ident", [128, 128], F32)
    identb, Bidentb = sb("identb", [128, 128], BF16)
    tri, Btri = sb("tri", [128, 128], F32)
    triBD, BtriBD = sb("triBD", [128, 128], F32)
    ones, Bones = sb("ones", [128, 128], F32)
    onehot0, Bonehot0 = sb("onehot0", [128, 128], F32)
    maskn, Bmaskn = sb("maskn", [128, 128], BF16)
    masknBD, BmasknBD = sb("masknBD", [128, 128], BF16)
    ones2e, Bones2e = sb("ones2e", [128, 128], BF16)
    ones2o, Bones2o = sb("ones2o", [128, 128], BF16)
    upper, Bupper = sb("upper", [128, 128], F32)
    vmask, Bvmask = sb("vmask", [128, 2], F32)

    pool(lambda e: e.memset(ident[:], 0.0), [], [Bident])
    pool(lambda e: e.affine_select(out=ident[:], in_=ident[:], pattern=[[-1, 128]], compare_op=ALU.not_equal,
                                   fill=1.0, base=0, channel_multiplier=1), [Bident], [Bident])
    pool(lambda e: e.tensor_copy(out=identb[:], in_=ident[:]), [Bident], [Bidentb])
    pool(lambda e: e.memset(ones[:], 1.0), [], [Bones])
    pool(lambda e: e.memset(onehot0[:], 0.0), [], [Bonehot0])
    pool(lambda e: e.memset(onehot0[0:1, :], 1.0), [Bonehot0], [Bonehot0])
    pool(lambda e: e.affine_select(out=tri[:], in_=ones[:], pattern=[[1, 128]], compare_op=ALU.is_ge,
                                   fill=0.0, base=0, channel_multiplier=-1), [Bones], [Btri])
    pool(lambda e: e.affine_select(out=upper[:], in_=ones[:], pattern=[[-1, 128]], compare_op=ALU.is_gt,
                                   fill=0.0, base=0, channel_multiplier=1), [Bones], [Bupper])
    pool(lambda e: e.tensor_copy(out=triBD[:], in_=tri[:]), [Btri], [BtriBD])
    pool(lambda e: e.memset(maskn[:], 0.0), [], [Bmaskn])
    pool(lambda e: e.affine_select(out=maskn[:], in_=maskn[:], pattern=[[1, 128]], compare_op=ALU.is_ge,
                                   fill=-30000.0, base=0, channel_multiplier=-1), [Bmaskn], [Bmaskn])
    pool(lambda e: e.tensor_copy(out=masknBD[:], in_=maskn[:]), [Bmaskn], [BmasknBD])
    for i in range(1, 4):
        def f1(e, i=i):
            return e.affine_select(out=triBD[:, 32 * i:32 * i + 32], in_=triBD[:, 32 * i:32 * i + 32],
                                   pattern=[[0, 32]], compare_op=ALU.is_ge, fill=0.0, base=-32 * i,
                                   channel_multiplier=1)
        pool(f1, [BtriBD], [BtriBD])

        def f2(e, i=i):
            return e.affine_select(out=masknBD[:, 32 * i:32 * i + 32], in_=masknBD[:, 32 * i:32 * i + 32],
                                   pattern=[[0, 32]], compare_op=ALU.is_ge, fill=-30000.0, base=-32 * i,
                                   channel_multiplier=1)
        pool(f2, [BmasknBD], [BmasknBD])
    pool(lambda e: e.memset(ones2e[:], 0.0), [], [Bones2e])
    pool(lambda e: e.memset(ones2e[:, 0:64], 1.0), [Bones2e], [Bones2e])
    pool(lambda e: e.memset(ones2o[:], 0.0), [], [Bones2o])
    pool(lambda e: e.memset(ones2o[:, 64:128], 1.0), [Bones2o], [Bones2o])
    pool(lambda e: e.memset(vmask[:, 0:1], 0.0), [], [Bvmask])
    pool(lambda e: e.memset(vmask[0:16, 0:1], 1.0), [Bvmask], [Bvmask])
    pool(lambda e: e.memset(vmask[:, 1:2], -1e30), [Bvmask], [Bvmask])
    pool(lambda e: e.memset(vmask[0:16, 1:2], 0.0), [Bvmask], [Bvmask])

    grep, Bgrep = sb("grep", [128, 1024], F32)
    gk_rep, Bgk_rep = sb("gk_rep", [128, 64], F32)
    gq2, Bgq2 = sb("gq2", [128, 1], F32)
    gk2, Bgk2 = sb("gk2", [128, 1], F32)
    bf_rep, Bbf_rep = sb("bf_rep", [128, 16], F32)
    S.dma("sp", [(gk_rep[:], fox_g_k[0, :].partition_broadcast(128))], D_in, Bgk_rep)
    S.dma("sp", [(gq2[0:64, :], fox_g_q[0, :].rearrange("(d o) -> d o", o=1)),
                 (gq2[64:128, :], fox_g_q[0, :].rearrange("(d o) -> d o", o=1))], D_in, Bgq2)
    S.dma("sp", [(gk2[0:64, :], fox_g_k[0, :].rearrange("(d o) -> d o", o=1)),
                 (gk2[64:128, :], fox_g_k[0, :].rearrange("(d o) -> d o", o=1))], D_in, Bgk2)
    S.dma("sp", [(bf_rep[:], fox_b_f[0, :].partition_broadcast(128))], D_in, Bbf_rep)

    WK = []
    for kc in range(8):
        t_ = es.enter_context(nc.sbuf_tensor("WK%d" % kc, [128, 4096], BF16))
        WK.append((t_, S.buf("WKa%d" % kc), S.buf("WKb%d" % kc)))

    def wkb(kc, c0, n):
        bs = []
        if c0 < 2048:
            bs.append(WK[kc][1])
        if c0 + n > 2048:
            bs.append(WK[kc][2])
        return bs

    def load_piece(src_ap, ncols, kc, dcol0):
        c = 0
        while c < ncols:
            d0 = dcol0 + c
            n = min(ncols - c, (2048 - d0) if d0 < 2048 else (4096 - d0))
            S.dma("pool", [(WK[kc][0][:, d0:d0 + n], src_ap[:, c:c + n])], D_in, wkb(kc, d0, n)[0], max_dma_last_dim=4096)
            c += n

    def load_w(w_ap, ncols, col0=0, dcol0=0):
        for kc in range(8):
            load_piece(w_ap[kc * 128:(kc + 1) * 128, col0:col0 + ncols], ncols, kc, dcol0)

    def load_w_down_half(w_ap, half):
        for fc in range(16):
            r0 = half * 2048 + fc * 128
            load_piece(w_ap[r0:r0 + 128, :], 1024, fc // 2, 2048 + (fc % 2) * 1024)

    def load_grep(src_row):
        S.dma("sp", [(grep[:], src_row.partition_broadcast(128))], D_in, Bgrep)

    xt_rr = sbs("xt", [128, 1024], F32, 3)
    junk, Bjunk = sb("junk", [128, 1024], BF16)
    hb_rr = sbs("hb", [128, 1024], F32, 2)
    hT_rr = sbs("hT", [128, 1024], BF16, 2)
    st_rr = sbs("stat", [128, 8], F32, 4)

    def rstd_from_ss(ss_ap, Bss, mean_div):
        dve(lambda e: e.tensor_scalar(out=ss_ap, in0=ss_ap, scalar1=1.0 / mean_div, scalar2=EPS,
                                      op0=ALU.mult, op1=ALU.add), [Bss], [Bss])
        act(lambda e: e.activation(out=ss_ap, in_=ss_ap, func=AF.Sqrt), [Bss], [Bss])
        dve(lambda e: e.reciprocal(out=ss_ap, in_=ss_ap), [Bss], [Bss])

    def rmsnorm_tile(x, Bx, out, Bout):
        st, Bst = st_rr.next()
        act(lambda e: e.activation(out=junk[:], in_=x, func=AF.Square, accum_out=st[:, 0:1]), [Bx], [Bjunk, Bst])
        rstd_from_ss(st[:, 0:1], Bst, 1024.0)
        dve(lambda e: e.scalar_tensor_tensor(out=out, in0=x, scalar=st[:, 0:1], in1=grep[:],
                                             op0=ALU.mult, op1=ALU.mult), [Bx, Bst, Bgrep], [Bout])

    def transpose_1024(src, Bsrc, dst_of_group, Bdst):
        for g in range(2):
            pt, Bpt = psrr.next()
            for j in range(4):
                kc = g * 4 + j
                pe(lambda e, kc=kc, j=j, pt=pt: e.transpose(out=pt[:, j * 128:(j + 1) * 128],
                                                            in_=src[:, kc * 128:(kc + 1) * 128], identity=ident[:]),
                   [Bsrc, Bident], [Bpt])
            o_ap = dst_of_group(g)
            i_ap = pt[:, 0:512] if len(o_ap.shape) == 2 else pt[:, 0:512].rearrange("p (j t) -> p j t", j=4)
            act(lambda e, o_ap=o_ap, i_ap=i_ap: e.activation(out=o_ap, in_=i_ap, func=AF.Copy), [Bpt], [Bdst])

    hbb_rr = sbs("hbb", [128, 1024], BF16, 2)

    def transpose_bf(src, Bsrc, dst_ap, Bdst):
        pt, Bpt = psrr.next()
        ptb = pt[:, 0:512].bitcast(BF16)
        for kc in range(8):
            pe(lambda e, kc=kc: e.transpose(out=ptb[:, kc * 128:(kc + 1) * 128], in_=src[:, kc * 128:(kc + 1) * 128], identity=identb[:]),
               [Bsrc, Bidentb], [Bpt])
        i_ap = ptb if len(dst_ap.shape) == 2 else ptb.rearrange("p (k t) -> p k t", k=8)
        act(lambda e: e.activation(out=dst_ap, in_=i_ap, func=AF.Copy), [Bpt], [Bdst])

    def norm_and_transpose(x, Bx):
        hb, Bhb = hbb_rr.next()
        rmsnorm_tile(x[:], Bx, hb[:], Bhb)
        hT, BhT = hT_rr.next()
        transpose_bf(hb, Bhb, hT[:], BhT)
        return hT, BhT

    def mm_group(hT, BhT, c0, ncols):
        p, Bp = psrr.next()
        for kc in range(8):
            pe(lambda e, kc=kc: e.matmul(p[:, 0:ncols], lhsT=hT[:, kc * 128:(kc + 1) * 128],
                                         rhs=WK[kc][0][:, c0:c0 + ncols], start=(kc == 0), stop=(kc == 7)),
               [BhT] + wkb(kc, c0, ncols), [Bp])
        return p, Bp

    def run_interleaved(gens, lag):
        gens = list(gens)
        active = []
        while gens or active:
            if gens and len(active) < 2 and (not active or active[-1][1] >= lag):
                active.append([gens.pop(0), 0])
            for a in list(active):
                try:
                    next(a[0])
                    a[1] += 1
                except StopIteration:
                    active.remove(a)

    def make_prefetcher(order, loader):
        order = list(order)
        cache = {}

        def get(ti):
            if ti not in cache:
                cache[ti] = loader(ti)
            h = cache.pop(ti)
            k = order.index(ti)
            if k + 1 < len(order) and order[k + 1] not in cache:
                cache[order[k + 1]] = loader(order[k + 1])
            return h
        return get

    def load_x(ti):
        x, Bx = xt_rr.next()
        S.dma("sp", [(x[:], xs_d[ti * 128:(ti + 1) * 128, :])], D_xs[ti], Bx)
        return x, Bx

    def store_x(ti, x, Bx):
        S.dma("sp", [(xs_d[ti * 128:(ti + 1) * 128, :], x[:])], Bx, D_xs[ti])

    nF, BnF = sb("nF", [128, NT * 16], F32)
    car, Bcar = sb("car", [128, 16], F32)
    revb, Brevb = sb("revb", [128, 4 * PB * 16], F32)
    tile_order = [TM] + list(range(FT)) + [TS]

    def phase_A():
        load_w(fox_w_in, 3088)
        load_grep(g_mix[0, :])
        pool(lambda e: e.memset(car[:], 0.0), [], [Bcar])
        sq_rr = sbs("sq", [128, 512], F32, 2)
        kn_rr = sbs("kn", [128, 512], F32, 2)
        kf_rr = sbs("kf", [128, 1024], F32, 2)
        vf_rr = sbs("vf", [128, 1024], F32, 2)
        v2_rr = sbs("v2t", [128, 2048], BF16, 2)
        for (t_, B_) in v2_rr.items:
            pool(lambda e, t_=t_: e.memset(t_[:], 0.0), [], [B_])
        tT_rr = sbs("tT", [128, 512], BF16, 3)
        lf_rr = sbs("lf", [128, 16], F32, 2)
        z_rr = sbs("z", [128, 16], F32, 2)

        def qk_epilogue(p, Bp, half, ti, is_k, kf, Bkf):
            sq, Bsq = sq_rr.next()
            st, Bst = st_rr.next()
            kn, Bkn = kn_rr.next()
            act(lambda e: e.activation(out=sq[:], in_=p[:, 0:512], func=AF.Square), [Bp], [Bsq])
            dve(lambda e: e.tensor_reduce(out=st[:, 0:8], in_=sq[:].rearrange("p (h d) -> p h d", h=8),
                                          axis=AX.X, op=ALU.add), [Bsq], [Bst])
            rstd_from_ss(st[:, 0:8], Bst, 64.0)
            dve(lambda e: e.tensor_tensor(out=kn[:].rearrange("p (h d) -> p h d", h=8),
                                          in0=p[:, 0:512].rearrange("p (h d) -> p h d", h=8),
                                          in1=st[:, 0:8].unsqueeze(2).to_broadcast([128, 8, 64]), op=ALU.mult),
                [Bp, Bst], [Bkn])
            if is_k:
                dve(lambda e: e.tensor_tensor(out=kf[:, half * 512:(half + 1) * 512].rearrange("p (h d) -> p h d", h=8),
                                              in0=kn[:].rearrange("p (h d) -> p h d", h=8),
                                              in1=gk_rep[:].unsqueeze(1).to_broadcast([128, 8, 64]), op=ALU.mult),
                    [Bkn, Bgk_rep], [Bkf])
            yield
            pt, Bpt = psrr.next()
            for j in range(4):
                pe(lambda e, j=j: e.transpose(out=pt[:, j * 128:(j + 1) * 128], in_=kn[:, j * 128:(j + 1) * 128],
                                              identity=ident[:]), [Bkn, Bident], [Bpt])
            tT, BtT = tT_rr.next()
            g2, Bg2 = (gk2, Bgk2) if is_k else (gq2, Bgq2)
            act(lambda e: e.activation(out=tT[:], in_=pt[:, 0:512], func=AF.Copy, scale=g2[:, 0:1]), [Bpt, Bg2], [BtT])
            dst_d, Bdst = (kT_d, D_kT) if is_k else (qT_d, D_qT)
            S.dma("sp", [(dst_d[half * 4:(half + 1) * 4, :, ti * 128:(ti + 1) * 128].rearrange("j p t -> p j t"),
                          tT[:].rearrange("p (j t) -> p j t", j=4))], BtT, Bdst)

        def tileA(ti):
            x, Bx = getA(ti)
            store_x(ti, x, Bx)
            hT, BhT = norm_and_transpose(x, Bx)
            yield
            SK = os.environ.get('SK', '')
            kf, Bkf = kf_rr.next()
            for half in range(2):
                if 'q' in SK:
                    continue
                p, Bp = mm_group(hT, BhT, half * 512, 512)
                yield
                yield from qk_epilogue(p, Bp, half, ti, False, None, None)
                yield
            for half in range(2):
                if 'k' in SK:
                    continue
                p, Bp = mm_group(hT, BhT, 1024 + half * 512, 512)
                yield
                yield from qk_epilogue(p, Bp, half, ti, True, kf, Bkf)
                yield
            if 'k' not in SK:
                S.dma("sp", [(k_out[ti * 128:(ti + 1) * 128, :], kf[:])], Bkf, D_out)
            vf, Bvf = vf_rr.next()
            v2, Bv2 = v2_rr.next()
            for half in range(2):
                if 'v' in SK:
                    continue
                yield
                p, Bp = mm_group(hT, BhT, 2048 + half * 512, 512)
                act(lambda e, half=half, p=p: e.activation(out=vf[:, half * 512:(half + 1) * 512], in_=p[:, 0:512], func=AF.Copy),
                    [Bp], [Bvf])
                v2v = v2[:].rearrange("p (h c) -> p h c", h=16)
                pv = p[:, 0:512].rearrange("p (h two d) -> p h two d", two=2, d=64)
                act(lambda e, half=half, v2v=v2v, pv=pv: e.activation(out=v2v[:, half * 8:half * 8 + 8:2, 0:64], in_=pv[:, :, 0, :], func=AF.Copy),
                    [Bp], [Bv2])
                act(lambda e, half=half, v2v=v2v, pv=pv: e.activation(out=v2v[:, half * 8 + 1:half * 8 + 8:2, 64:128], in_=pv[:, :, 1, :], func=AF.Copy),
                    [Bp], [Bv2])
            if 'v' not in SK:
                S.dma("sp", [(v_out[ti * 128:(ti + 1) * 128, :], vf[:])], Bvf, D_out)
                S.dma("sp", [(v2_d[ti, :, :], v2[:])], Bv2, D_v2)
            if 'f' in SK:
                return
            yield
            p, Bp = mm_group(hT, BhT, 3072, 16)
            z, Bz = z_rr.next()
            lf, Blf = lf_rr.next()
            dve(lambda e: e.tensor_tensor(out=z[:], in0=p[:, 0:16], in1=bf_rep[:], op=ALU.add), [Bp, Bbf_rep], [Bz])
            act(lambda e: e.activation(out=z[:], in_=z[:], func=AF.Exp, scale=-1.0), [Bz], [Bz])
            act(lambda e: e.activation(out=z[:], in_=z[:], func=AF.Ln, bias=1.0), [Bz], [Bz])
            dve(lambda e: e.tensor_scalar(out=lf[:], in0=z[:], scalar1=-1.0, scalar2=None, op0=ALU.mult), [Bz], [Blf])
            S.dma("sp", [(lf_out[ti * 128:(ti + 1) * 128, :], lf[:])], Blf, D_out)
            if ti == TM:
                dve(lambda e: e.tensor_scalar(out=z[:], in0=lf[:], scalar1=vmask[:, 0:1], scalar2=None, op0=ALU.mult),
                    [Blf, Bvmask], [Bz])
                lfc, Blfc = z, Bz
            else:
                lfc, Blfc = lf, Blf
            pf, Bpf = psrr.next()
            if ti == TS:
                pe(lambda e: e.matmul(pf[:, 0:16], lhsT=triBD[:], rhs=lfc[:], start=True, stop=True), [BtriBD, Blfc], [Bpf])
                dve(lambda e: e.tensor_scalar(out=nF[:, ti * 16:(ti + 1) * 16], in0=pf[:, 0:16], scalar1=-1.0, scalar2=None,
                                              op0=ALU.mult), [Bpf], [BnF])
            else:
                pe(lambda e: e.matmul(pf[:, 0:16], lhsT=tri[:], rhs=lfc[:], start=True, stop=False), [Btri, Blfc], [Bpf])
                pe(lambda e: e.matmul(pf[:, 16:32], lhsT=ones[:], rhs=lfc[:], start=False, stop=True), [Bones, Blfc], [Bpf])
                dve(lambda e: e.scalar_tensor_tensor(out=nF[:, ti * 16:(ti + 1) * 16], in0=pf[:, 0:16], scalar=-1.0,
                                                     in1=car[:], op0=ALU.mult, op1=ALU.subtract), [Bpf, Bcar], [BnF])
                dve(lambda e: e.tensor_tensor(out=car[:], in0=pf[:, 16:32], in1=car[:], op=ALU.add), [Bpf, Bcar], [Bcar])

        def loadA(ti):
            x, Bx = xt_rr.next()
            S.dma("sp", [(x[:], xin[ti * 128:(ti + 1) * 128, :])], D_in, Bx)
            return x, Bx
        getA = make_prefetcher(tile_order, loadA)
        run_interleaved([tileA(ti) for ti in tile_order], lag=8)

        clall, Bclall = sb("clall", [128, 4 * PB * 16], F32)
        clv = clall[:].rearrange("p (i b h) -> p i b h", i=4, b=PB)
        S.dma("sp", [(clv[:, i, :, :], cl[i].rearrange("(b p) h -> p b h", p=128)) for i in range(4)], D_in, Bclall)
        revv = revb[:].rearrange("p (i b h) -> p i b h", i=4, b=PB)
        ckt_rr = sbs("ckt", [128, 1024], F32, 2)
        cvt_rr = sbs("cvt", [128, 1024], F32, 2)
        kcs_rr = sbs("kcs", [128, 1024], BF16, 2)

        def cache_block(i, b):
            ckt, Bckt = ckt_rr.next()
            cvt, Bcvt = cvt_rr.next()
            S.dma("sp", [(ckt[:], ck[i, b * 128:(b + 1) * 128, :])], D_in, Bckt)
            S.dma("sp", [(cvt[:], cv[i, b * 128:(b + 1) * 128, :])], D_in, Bcvt)
            yield
            kcs, Bkcs = kcs_rr.next()
            transpose_1024(ckt, Bckt, lambda g: kcs[:, g * 512:(g + 1) * 512], Bkcs)
            S.dma("sp", [(kcT_d[i, :, :, b * 128:(b + 1) * 128].rearrange("j p t -> p j t"),
                          kcs[:].rearrange("p (j t) -> p j t", j=8))], Bkcs, D_kcT)
            yield
            v2, Bv2 = v2_rr.next()
            v2v = v2[:].rearrange("p (h c) -> p h c", h=16)
            cvv = cvt[:].rearrange("p (h two d) -> p h two d", two=2, d=64)
            dve(lambda e: e.tensor_copy(out=v2v[:, 0:16:2, 0:64], in_=cvv[:, :, 0, :]), [Bcvt], [Bv2])
            dve(lambda e: e.tensor_copy(out=v2v[:, 1:16:2, 64:128], in_=cvv[:, :, 1, :]), [Bcvt], [Bv2])
            S.dma("sp", [(v2c_d[i, b, :, :], v2[:])], Bv2, D_v2c)

        def rev_bias(i):
            rc, Brc = sb("rcar%d" % i, [128, 16], F32)
            pool(lambda e: e.memset(rc[:], 0.0), [], [Brc])
            for b in reversed(range(PB)):
                pf, Bpf = psrr.next()
                pe(lambda e, b=b, pf=pf: e.matmul(pf[:, 0:16], lhsT=upper[:], rhs=clv[:, i, b, :], start=True, stop=False), [Bupper, Bclall], [Bpf])
                pe(lambda e, b=b, pf=pf: e.matmul(pf[:, 16:32], lhsT=ones[:], rhs=clv[:, i, b, :], start=False, stop=True), [Bones, Bclall], [Bpf])
                dve(lambda e, b=b, pf=pf: e.tensor_tensor(out=revv[:, i, b, :], in0=pf[:, 0:16], in1=rc[:], op=ALU.add), [Bpf, Brc], [Brevb])
                dve(lambda e, pf=pf: e.tensor_tensor(out=rc[:], in0=pf[:, 16:32], in1=rc[:], op=ALU.add), [Bpf, Brc], [Brc])
        for i in range(4):
            rev_bias(i)

        run_interleaved([cache_block(i, b_) for i in range(4) for b_ in range(PB)], lag=2)

    if "A" in PH:
        run_phase(phase_A)

    def phase_B():
        nFv = nF[:].rearrange("p (t h) -> p t h", h=16)
        revv = revb[:].rearrange("p (i b h) -> p i b h", i=4, b=PB)
        Fr, BFr = sb("Fr", [128, FT * 16], F32)
        pfr, Bpfr = psrr.next()
        for qi in range(FT):
            pe(lambda e, qi=qi: e.matmul(pfr[:, qi * 16:(qi + 1) * 16], lhsT=onehot0[:], rhs=nFv[:, qi, :],
                                         start=(qi == 0), stop=(qi == FT - 1)), [Bonehot0, BnF], [Bpfr])
        act(lambda e: e.activation(out=Fr[:], in_=pfr[:, 0:FT * 16], func=AF.Copy), [Bpfr], [BFr])
        Frv = Fr[:].rearrange("p (t h) -> p t h", h=16)

        KT, BKT = sb("KTp", [128, R], BF16)
        QT, BQT = sb("QTp", [128, R], BF16)
        V2p, BV2p = sb("V2p", [128, NT * 256], BF16)
        V2v = V2p[:].rearrange("p (t c) -> p t c", c=256)
        KcT_rr = sbs("KcT", [128, PB * 128], BF16, 3)
        V2c_rr = sbs("V2c", [128, PB * 256], BF16, 3)
        bias_rr = sbs("biasq", [128, NT * 16], F32, 2)
        SQ = min(4, FT)
        pt_rr = sbs("ptb", [128, 512], BF16, 4)
        rden_rr = sbs("rden", [128, 512], F32, 2)
        oTt_rr = sbs("oTt", [128, 512], BF16, 2)
        ones2 = [(ones2e, Bones2e), (ones2o, Bones2o)]
        acc_rr = RR(PS[0:4])
        sc_rr = RR(PS[4:8])

        items = []
        LA = 2

        def block(p, e_, qcol0, nq, kT_ap, nk, v_ap, bias_ap, Bbias, mask, Bmask, o_ps, Bo, d_ps, Bd, ocol0, first, last, Bk, Bv, pre=None):
            r0 = e_ * 64
            st = {}

            def stage1():
                if pre is not None:
                    pre()
                ps_s, Bs = sc_rr.next()
                pe(lambda e: e.matmul(ps_s[0:nk, 0:nq], lhsT=kT_ap, rhs=QT[r0:r0 + 64, qcol0:qcol0 + nq],
                                      start=True, stop=(mask is None)), [Bk, BQT], [Bs])
                if mask is not None:
                    mq = min(nq, 128)
                    pe(lambda e: e.matmul(ps_s[0:nk, 0:mq], lhsT=identb[0:nk, 0:nk], rhs=mask[0:nk, 0:mq],
                                          start=False, stop=True), [Bidentb, Bmask], [Bs])
                ptb, Bptb = pt_rr.next()
                act(lambda e: e.activation(out=ptb[0:nk, 0:nq], in_=ps_s[0:nk, 0:nq], func=AF.Exp, bias=bias_ap, scale=0.125),
                    [Bs, Bbias], [Bptb])
                st["pt"] = (ptb, Bptb)

            def stage2():
                ptb, Bptb = st["pt"]
                o2, Bo2 = ones2[e_]
                pe(lambda e: e.matmul(o_ps[:, ocol0:ocol0 + nq], lhsT=v_ap, rhs=ptb[0:nk, 0:nq], start=first, stop=last),
                   [Bv, Bptb], [Bo])
                pe(lambda e: e.matmul(d_ps[:, ocol0:ocol0 + nq], lhsT=o2[0:nk, :], rhs=ptb[0:nk, 0:nq], start=first, stop=last),
                   [Bo2, Bptb], [Bd])
            items.append([stage1, stage2, None])

        def flush():
            n = len(items)
            for i in range(n + LA):
                if i < n:
                    items[i][0]()
                if i - LA >= 0:
                    items[i - LA][1]()
                    if items[i - LA][2] is not None:
                        items[i - LA][2]()
            del items[:]

        def finish(p, q0, W, o_ps, Bo, d_ps, Bd):
            items[-1][2] = lambda: finish_now(p, q0, W, o_ps, Bo, d_ps, Bd)

        def finish_now(p, q0, W, o_ps, Bo, d_ps, Bd):
            rden, Brden = rden_rr.next()
            oTt, BoTt = oTt_rr.next()
            dve(lambda e: e.reciprocal(out=rden[:, 0:W], in_=d_ps[:, 0:W]), [Bd], [Brden])
            dve(lambda e: e.tensor_tensor(out=oTt[:, 0:W], in0=o_ps[:, 0:W], in1=rden[:, 0:W], op=ALU.mult), [Bo, Brden], [BoTt])
            S.dma("sp", [(oT_d[p, :, q0:q0 + W], oTt[:, 0:W])], BoTt, D_oT)

        for p in range(8):
            S.dma("sp", [(KT[:], kT_d[p, :, :])], D_kT, BKT)
            S.dma("sp", [(QT[:], qT_d[p, :, :])], D_qT, BQT)
            S.dma("sp", [(V2v, v2_d[:, :, p * 256:(p + 1) * 256].rearrange("t s c -> s t c"))], D_v2, BV2p)
            units = []
            for e_ in range(2):
                for i in range(4):
                    KcT, BKcT = KcT_rr.next()
                    V2c, BV2c = V2c_rr.next()
                    V2cv = V2c[:].rearrange("p (b c) -> p b c", c=256)

                    def mkload(KcT=KcT, BKcT=BKcT, V2cv=V2cv, BV2c=BV2c, i=i, p=p):
                        S.dma("sp", [(KcT[:], kcT_d[i, p, :, :])], D_kcT, BKcT)
                        S.dma("sp", [(V2cv, v2c_d[i, :, :, p * 256:(p + 1) * 256].rearrange("b s c -> s b c"))], D_v2c, BV2c)
                    units.append((KcT, BKcT, V2cv, BV2c, mkload))
            o_ps, Bo = acc_rr.next()
            d_ps, Bd = acc_rr.next()
            for e_ in range(2):
                h = 2 * p + e_
                block(p, e_, TM * 128, 128, KT[e_ * 64:(e_ + 1) * 64, TM * 128:(TM + 1) * 128], 128,
                      V2v[:, TM, e_ * 128:(e_ + 1) * 128], nFv[:, TM, h:h + 1], BnF,
                      maskn, Bmaskn, o_ps, Bo, d_ps, Bd, 0, e_ == 0, e_ == 1, BKT, BV2p,
                      pre=(units[0][4] if e_ == 0 else None))
            finish(p, TM * 128, 128, o_ps, Bo, d_ps, Bd)
            for j in range(FT // SQ):
                W = SQ * 128
                q0 = j * W
                mid = j * SQ + SQ // 2
                bq, Bbq = bias_rr.next()
                bqv = bq[:].rearrange("p (t h) -> p t h", h=16)

                def mkbias(bqv=bqv, mid=mid, Bbq=Bbq):
                    dve(lambda e: e.tensor_tensor(out=bqv, in0=nFv, in1=Frv[:, mid, :].unsqueeze(1).to_broadcast([128, NT, 16]),
                                                  op=ALU.subtract), [BnF, BFr], [Bbq])
                o_ps, Bo = acc_rr.next()
                d_ps, Bd = acc_rr.next()
                nblk = 2 * (1 + j * SQ + SQ)
                cnt = 0
                for e_ in range(2):
                    h = 2 * p + e_
                    block(p, e_, q0, W, KT[e_ * 64:(e_ + 1) * 64, TM * 128:TM * 128 + 16], 16,
                          V2v[0:16, TM, e_ * 128:(e_ + 1) * 128], bqv[0:16, TM, h:h + 1], Bbq,
                          None, None, o_ps, Bo, d_ps, Bd, 0, cnt == 0, cnt == nblk - 1, BKT, BV2p,
                          pre=(mkbias if e_ == 0 else None))
                    cnt += 1
                    for kt in range(j * SQ):
                        block(p, e_, q0, W, KT[e_ * 64:(e_ + 1) * 64, kt * 128:(kt + 1) * 128], 128,
                              V2v[:, kt, e_ * 128:(e_ + 1) * 128], bqv[:, kt, h:h + 1], Bbq,
                              None, None, o_ps, Bo, d_ps, Bd, 0, cnt == 0, cnt == nblk - 1, BKT, BV2p)
                        cnt += 1
                    for jj in range(SQ):
                        kt = j * SQ + jj
                        block(p, e_, q0 + jj * 128, W - jj * 128, KT[e_ * 64:(e_ + 1) * 64, kt * 128:(kt + 1) * 128], 128,
                              V2v[:, kt, e_ * 128:(e_ + 1) * 128], bqv[:, kt, h:h + 1], Bbq,
                              maskn, Bmaskn, o_ps, Bo, d_ps, Bd, jj * 128, cnt == 0, cnt == nblk - 1, BKT, BV2p)
                        cnt += 1
                finish(p, q0, W, o_ps, Bo, d_ps, Bd)
            o_ps, Bo = acc_rr.next()
            d_ps, Bd = acc_rr.next()
            nblk = 2 * (1 + 4 * PB)
            cnt = 0
            for e_ in range(2):
                h = 2 * p + e_
                block(p, e_, TS * 128, 128, KT[e_ * 64:(e_ + 1) * 64, TS * 128:(TS + 1) * 128], 128,
                      V2v[:, TS, e_ * 128:(e_ + 1) * 128], nFv[:, TS, h:h + 1], BnF,
                      masknBD, BmasknBD, o_ps, Bo, d_ps, Bd, 0, cnt == 0, cnt == nblk - 1, BKT, BV2p)
                cnt += 1
                for i in range(4):
                    u = e_ * 4 + i
                    KcT, BKcT, V2cv, BV2c, _ = units[u]
                    nxt = units[u + 1][4] if u + 1 < 8 else None
                    for b in range(PB):
                        block(p, e_, TS * 128 + 32 * i, 32, KcT[e_ * 64:(e_ + 1) * 64, b * 128:(b + 1) * 128], 128,
                              V2cv[:, b, e_ * 128:(e_ + 1) * 128], revv[:, i, b, h:h + 1], Brevb,
                              None, None, o_ps, Bo, d_ps, Bd, 32 * i, cnt == 0, cnt == nblk - 1, BKcT, BV2c,
                              pre=(nxt if b == 0 else None))
                        cnt += 1
            finish(p, TS * 128, 128, o_ps, Bo, d_ps, Bd)
            flush()

    def phase_C():
        load_w_down_half(w_down[0], 0)
        oTl_rr = sbs("oTl", [128, 1024], BF16, 3)
        def tileC(ti):
            oTl, BoTl, x, Bx = getC(ti)
            yield
            for half in range(2):
                p_, Bp_ = mm_group(oTl, BoTl, half * 512, 512)
                dve(lambda e, half=half, p_=p_, x=x: e.tensor_tensor(out=x[:, half * 512:(half + 1) * 512], in0=p_[:, 0:512],
                                                                    in1=x[:, half * 512:(half + 1) * 512], op=ALU.add),
                    [Bp_, Bx], [Bx])
                yield
            store_x(ti, x, Bx)
        def loadC(ti):
            oTl, BoTl = oTl_rr.next()
            S.dma("sp", [(oTl[:].rearrange("p (j t) -> p j t", j=8),
                          oT_d[:, :, ti * 128:(ti + 1) * 128].rearrange("j p t -> p j t"))], D_oT, BoTl)
            x, Bx = load_x(ti)
            return oTl, BoTl, x, Bx
        getC = make_prefetcher(range(NT), loadC)
        run_interleaved([tileC(ti) for ti in range(NT)], lag=2)

    if "C" in PH:
        load_w(fox_w_out, 1024)
    if "B" in PH:
        run_phase(phase_B)
    if "C" in PH:
        run_phase(phase_C)


    groups = [list(range(g * 4, min(FT, g * 4 + 4))) for g in range((FT + 3) // 4)] + [[TS, TM]]
    hTs_d = dscr("hTs", [len(groups), 128, 4096], BF16); D_hTs = S.buf("D_hTs", multi=True)

    def phase_FFN(l, half, final):
        load_w(w_up[l], 2048, col0=half * 2048, dcol0=0)
        if half == 1:
            load_w_down_half(w_down[l], half)
        if half == 0:
            load_grep(g_ffn[l, :])
        elif final:
            load_grep(g_final[0, :])
        xg_rr = sbs("xg", [128, 4096], F32, 2)
        hT4_rr = sbs("hT4", [128, 4096], BF16, 2)
        aT, BaT = sb("aT", [128, 16 * 512], BF16)
        rt_rr = sbs("rt", [128, 512], F32, 2)
        yo_rr = sbs("yo", [128, 1024], F32, 2) if final else None
        if os.environ.get('BURN') == '1':
            xg_rr.next(); hT4_rr.next()
        def loadF(gi):
            tiles = groups[gi]
            xg, Bxg = xg_rr.next()
            hT4, BhT4 = hT4_rr.next()
            S.dma("sp", [(xg[:, s_ * 1024:(s_ + 1) * 1024], xs_d[ti * 128:(ti + 1) * 128, :]) for s_, ti in enumerate(tiles)],
                  D_xs[tiles[0]], Bxg, extra_reads=[D_xs[t] for t in tiles[1:]])
            if half == 1:
                S.dma("sp", [(hT4[:], hTs_d[gi, :, :])], D_hTs, BhT4)
            return xg, Bxg, hT4, BhT4
        getF = make_prefetcher(range(len(groups)), loadF)
        order = list(enumerate(groups))
        if os.environ.get('GORD') == '1':
            order = order[::-1]
        def do_group(gi, tiles):
                n = len(tiles)
                GW = 128 * n
                xg, Bxg, hT4, BhT4 = getF(gi)
                hT4v = hT4[:].rearrange("p (k t) -> p k t", k=8)
                if half == 0:
                    for s_, ti in enumerate(tiles):
                        hb, Bhb = hbb_rr.next()
                        rmsnorm_tile(xg[:, s_ * 1024:(s_ + 1) * 1024], Bxg, hb[:], Bhb)
                        transpose_bf(hb, Bhb, hT4v[:, :, s_ * 128:(s_ + 1) * 128], BhT4)
                        if os.environ.get('DBGH') == '1':
                            dve(lambda e, hb=hb, s_=s_, xg=xg: e.tensor_copy(out=xg[:, s_ * 1024:(s_ + 1) * 1024], in_=hb[:]), [Bhb], [Bxg])
                    S.dma("sp", [(hTs_d[gi, :, :], hT4[:])], BhT4, D_hTs)
                for fc in range(16 if os.environ.get('DBGH') != '1' else 0):
                    pu, Bpu = psrr.next()
                    for kc in range(8):
                        pe(lambda e, kc=kc, fc=fc, pu=pu: e.matmul(pu[:, 0:GW], lhsT=WK[kc][0][:, fc * 128:(fc + 1) * 128],
                                                                   rhs=hT4v[:, kc, 0:GW], start=(kc == 0), stop=(kc == 7)),
                           [BhT4, WK[kc][1]], [Bpu])
                    rt, Brt = rt_rr.next()
                    if os.environ.get('FFNV', 'a') == 'a':
                        act(lambda e, pu=pu, rt=rt: e.activation(out=rt[:, 0:GW], in_=pu[:, 0:GW], func=AF.Relu), [Bpu], [Brt])
                        dve(lambda e, pu=pu, rt=rt, fc=fc: e.tensor_tensor(out=aT[:, fc * 512:fc * 512 + GW], in0=rt[:, 0:GW], in1=pu[:, 0:GW],
                                                                          op=ALU.mult), [Brt, Bpu], [BaT])
                    else:
                        dve(lambda e, pu=pu, rt=rt: e.tensor_scalar(out=rt[:, 0:GW], in0=pu[:, 0:GW], scalar1=0.0, scalar2=None, op0=ALU.max),
                            [Bpu], [Brt])
                        act(lambda e, rt=rt, fc=fc: e.activation(out=aT[:, fc * 512:fc * 512 + GW], in_=rt[:, 0:GW], func=AF.Square), [Brt], [BaT])
                for s_, ti in enumerate(tiles):
                    for dh in range(2 if os.environ.get('DBGH') != '1' else 0):
                        py, Bpy = psrr.next()
                        for fc in range(16):
                            pe(lambda e, fc=fc, py=py, s_=s_, dh=dh: e.matmul(
                                py[:, 0:512], lhsT=aT[:, fc * 512 + s_ * 128:fc * 512 + (s_ + 1) * 128],
                                rhs=WK[fc // 2][0][:, 2048 + (fc % 2) * 1024 + dh * 512:2048 + (fc % 2) * 1024 + (dh + 1) * 512],
                                start=(fc == 0), stop=(fc == 15)), [BaT, WK[fc // 2][2]], [Bpy])
                        c0 = s_ * 1024 + dh * 512
                        dve(lambda e, py=py, c0=c0, xg=xg: e.tensor_tensor(out=xg[:, c0:c0 + 512], in0=py[:, 0:512], in1=xg[:, c0:c0 + 512],
                                                                          op=ALU.add), [Bpy, Bxg], [Bxg])
                    S.dma("sp", [(xs_d[ti * 128:(ti + 1) * 128, :], xg[:, s_ * 1024:(s_ + 1) * 1024])], Bxg, D_xs[ti])
                    if final and half == 1:
                        yo, Byo = yo_rr.next()
                        rmsnorm_tile(xg[:, s_ * 1024:(s_ + 1) * 1024], Bxg, yo[:], Byo)
                        S.dma("sp", [(y_out[ti * 128:(ti + 1) * 128, :], yo[:])], Byo, D_out)


        for gi, tiles in order:
            do_group(gi, tiles)

    if "D" in PH:
        run_phase(phase_FFN, 0, 0, False)
        if os.environ.get("FFNH", "1") == "1":
            run_phase(phase_FFN, 0, 1, False)

    NCH = 1 + 2 * FT + 4
    mq_d = dscr("mq", [R, 512], F32); mk_d = dscr("mk", [R, 512], F32)
    mv_d = dscr("mv", [R, 1024], BF16); mo_d = dscr("mo", [R, 1024], F32)
    tok_d = dscr("tok", [R, 20], F32); hg_d = dscr("hg", [R, 1024], BF16)
    D_ml = S.buf("D_ml", multi=True)
    D_hg = S.buf("D_hg", multi=True)
    dec_row, Bdec = sb("dec_row", [4, NCH], F32)

    def phase_E():
        load_w(ml_w_in, 3080)
        load_grep(g_mix[1, :])
        bi_rep, Bbi = sb("bi_rep", [128, 4], F32)
        bfm_rep, Bbfm = sb("bfm_rep", [128, 4], F32)
        m0r, Bm0r = sb("m0r", [4, 4], F32)
        S.dma("sp", [(bi_rep[:], ml_b_i[0, :].partition_broadcast(128))], D_in, Bbi)
        S.dma("sp", [(bfm_rep[:], ml_b_f[0, :].partition_broadcast(128))], D_in, Bbfm)
        m0c, Bm0c = sb("m0c", [4, 4], F32)
        S.dma("sp", [(m0c[:], mm0[:, :])], D_in, Bm0c)
        pm, Bpm = psrr.next()
        pe(lambda e: e.transpose(out=pm[0:4, 0:4], in_=m0c[0:4, 0:4], identity=ident[0:4, 0:4]), [Bm0c, Bident], [Bpm])
        act(lambda e: e.activation(out=m0r[:], in_=pm[0:4, 0:4], func=AF.Copy), [Bpm], [Bm0r])
        car4, Bcar4 = sb("car4", [128, 4], F32)
        gcar, Bgcar = sb("gcar", [4, 1], F32)
        pool(lambda e: e.memset(car4[:], 0.0), [], [Bcar4])
        pool(lambda e: e.memset(gcar[:], 0.0), [], [Bgcar])
        qf_rr = sbs("qf", [128, 512], F32, 2)
        kf2_rr = sbs("kf2", [128, 512], F32, 2)
        vb_rr = sbs("vb", [128, 1024], BF16, 2)
        of_rr = sbs("of", [128, 1024], F32, 2)
        g8_rr = sbs("g8", [128, 16], F32, 2)
        row_rr = sbs("rowt", [4, 6 * 128], F32, 2)
        tokv_rr = sbs("tokv", [128, 24], F32, 2)

        def tileE(ti):
            x, Bx = getE(ti)
            hT, BhT = norm_and_transpose(x, Bx)
            yield
            rows = slice(ti * 128, (ti + 1) * 128)
            qf, Bqf = qf_rr.next()
            p, Bp = mm_group(hT, BhT, 0, 512)
            act(lambda e: e.activation(out=qf[:], in_=p[:, 0:512], func=AF.Copy), [Bp], [Bqf])
            S.dma("sp", [(mq_d[rows, :], qf[:])], Bqf, D_ml)
            yield
            kf2, Bkf2 = kf2_rr.next()
            p2, Bp2 = mm_group(hT, BhT, 512, 512)
            act(lambda e: e.activation(out=kf2[:], in_=p2[:, 0:512], func=AF.Copy, scale=float(128.0 ** -0.5)), [Bp2], [Bkf2])
            S.dma("sp", [(mk_d[rows, :], kf2[:])], Bkf2, D_ml)
            yield
            vb, Bvb = vb_rr.next()
            of, Bof = of_rr.next()
            for half in range(2):
                p3, Bp3 = mm_group(hT, BhT, 1024 + half * 512, 512)
                act(lambda e, half=half, p3=p3: e.activation(out=vb[:, half * 512:(half + 1) * 512], in_=p3[:, 0:512], func=AF.Copy),
                    [Bp3], [Bvb])
            yield
            for half in range(2):
                p4, Bp4 = mm_group(hT, BhT, 2048 + half * 512, 512)
                act(lambda e, half=half, p4=p4: e.activation(out=of[:, half * 512:(half + 1) * 512], in_=p4[:, 0:512], func=AF.Sigmoid),
                    [Bp4], [Bof])
            S.dma("sp", [(mv_d[rows, :], vb[:])], Bvb, D_ml)
            S.dma("sp", [(mo_d[rows, :], of[:])], Bof, D_ml)
            yield
            pg, Bpg = mm_group(hT, BhT, 3072, 8)
            g8, Bg8 = g8_rr.next()
            dve(lambda e: e.tensor_tensor(out=g8[:, 0:4], in0=pg[:, 0:4], in1=bi_rep[:], op=ALU.add), [Bpg, Bbi], [Bg8])
            dve(lambda e: e.tensor_tensor(out=g8[:, 4:8], in0=pg[:, 4:8], in1=bfm_rep[:], op=ALU.add), [Bpg, Bbfm], [Bg8])
            act(lambda e: e.activation(out=g8[:, 4:8], in_=g8[:, 4:8], func=AF.Exp, scale=-1.0), [Bg8], [Bg8])
            act(lambda e: e.activation(out=g8[:, 4:8], in_=g8[:, 4:8], func=AF.Ln, bias=1.0), [Bg8], [Bg8])
            dve(lambda e: e.tensor_scalar(out=g8[:, 4:8], in0=g8[:, 4:8], scalar1=-1.0, scalar2=None, op0=ALU.mult), [Bg8], [Bg8])
            if ti == TM:
                dve(lambda e: e.tensor_scalar(out=g8[:, 4:8], in0=g8[:, 4:8], scalar1=vmask[:, 0:1], scalar2=None, op0=ALU.mult),
                    [Bg8, Bvmask], [Bg8])
                dve(lambda e: e.tensor_scalar(out=g8[:, 0:4], in0=g8[:, 0:4], scalar1=vmask[:, 0:1], scalar2=vmask[:, 1:2],
                                              op0=ALU.mult, op1=ALU.add), [Bg8, Bvmask], [Bg8])
            pf, Bpf = psrr.next()
            if ti == TS:
                pe(lambda e: e.matmul(pf[:, 0:4], lhsT=triBD[:], rhs=g8[:, 4:8], start=True, stop=True), [BtriBD, Bg8], [Bpf])
                dve(lambda e: e.tensor_copy(out=g8[:, 8:12], in_=pf[:, 0:4]), [Bpf], [Bg8])
            else:
                pe(lambda e: e.matmul(pf[:, 0:4], lhsT=tri[:], rhs=g8[:, 4:8], start=True, stop=False), [Btri, Bg8], [Bpf])
                pe(lambda e: e.matmul(pf[:, 4:8], lhsT=ones[:], rhs=g8[:, 4:8], start=False, stop=True), [Bones, Bg8], [Bpf])
                dve(lambda e: e.tensor_tensor(out=g8[:, 8:12], in0=pf[:, 0:4], in1=car4[:], op=ALU.add), [Bpf, Bcar4], [Bg8])
                dve(lambda e: e.tensor_tensor(out=car4[:], in0=pf[:, 4:8], in1=car4[:], op=ALU.add), [Bpf, Bcar4], [Bcar4])
            dve(lambda e: e.tensor_tensor(out=g8[:, 12:16], in0=g8[:, 0:4], in1=g8[:, 8:12], op=ALU.subtract), [Bg8], [Bg8])
            yield
            rw, Brw = row_rr.next()
            A_r, G_r, nG_r = rw[:, 0:128], rw[:, 128:256], rw[:, 256:384]
            wg_r, rs_r, wi_r = rw[:, 384:512], rw[:, 512:640], rw[:, 640:768]
            pa, Bpa = psrr.next()
            pe(lambda e: e.transpose(out=pa[0:4, 0:128], in_=g8[:, 12:16], identity=ident[:]), [Bg8, Bident], [Bpa])
            act(lambda e: e.activation(out=A_r, in_=pa[0:4, 0:128], func=AF.Copy), [Bpa], [Brw])
            if ti == TS:
                segs = [(32 * i, 32 * i + 32, m0r[:, i:i + 1], Bm0r) for i in range(4)]
                for (c0, c1, init, Binit) in segs:
                    dve(lambda e, c0=c0, c1=c1, init=init: e.tensor_tensor_scan(out=G_r[:, c0:c1], data0=A_r[:, c0:c1], data1=A_r[:, c0:c1],
                                                                                 initial=init, op0=ALU.max, op1=ALU.max), [Brw, Binit], [Brw])
                chunks = [(32 * i, 32 * i + 32, m0r[:, i:i + 1], 1 + 2 * FT + i) for i in range(4)]
            else:
                dve(lambda e: e.tensor_tensor_scan(out=G_r, data0=A_r, data1=A_r, initial=gcar[:, 0:1], op0=ALU.max, op1=ALU.max),
                    [Brw, Bgcar], [Brw])
                cb = 0 if ti == TM else 1 + 2 * ti
                chunks = [(0, 64, gcar[:, 0:1], cb), (64, 128, G_r[:, 63:64], cb + 1 if ti != TM else None)]
            dve(lambda e: e.tensor_scalar(out=nG_r, in0=G_r, scalar1=-1.0, scalar2=None, op0=ALU.mult), [Brw], [Brw])
            yield
            for (c0, c1, gp, cidx) in chunks:
                act(lambda e, c0=c0, c1=c1: e.activation(out=wg_r[:, c0:c1], in_=A_r[:, c0:c1], func=AF.Exp, bias=nG_r[:, c1 - 1:c1]),
                    [Brw], [Brw])
                act(lambda e, c0=c0, c1=c1: e.activation(out=rs_r[:, c0:c1], in_=nG_r[:, c0:c1], func=AF.Exp, bias=G_r[:, c1 - 1:c1]),
                    [Brw], [Brw])
                act(lambda e, c0=c0, c1=c1, gp=gp: e.activation(out=wi_r[:, c0:c1], in_=nG_r[:, c0:c1], func=AF.Exp, bias=gp),
                    [Brw, Bgcar, Bm0r], [Brw])
                if cidx is not None:
                    act(lambda e, c1=c1, cidx=cidx: e.activation(out=dec_row[:, cidx:cidx + 1], in_=wi_r[:, c1 - 1:c1], func=AF.Copy),
                        [Brw], [Bdec])
            if ti != TS:
                act(lambda e: e.activation(out=gcar[:, 0:1], in_=G_r[:, 127:128], func=AF.Copy), [Brw], [Bgcar])
            yield
            pb, Bpb = psrr.next()
            for j, src in enumerate((wg_r, rs_r, wi_r, G_r)):
                pe(lambda e, j=j, src=src: e.transpose(out=pb[:, j * 4:(j + 1) * 4], in_=src, identity=ident[0:4, 0:4]),
                   [Brw, Bident], [Bpb])
            tokv, Btokv = tokv_rr.next()
            act(lambda e: e.activation(out=tokv[:, 0:16], in_=pb[:, 0:16], func=AF.Copy), [Bpb], [Btokv])
            dve(lambda e: e.tensor_tensor(out=tokv[:, 20:24], in0=tokv[:, 12:16], in1=g8[:, 8:12], op=ALU.add), [Btokv, Bg8], [Btokv])
            act(lambda e: e.activation(out=tokv[:, 16:20], in_=tokv[:, 20:24], func=AF.Exp, scale=-1.0), [Btokv], [Btokv])
            S.dma("sp", [(tok_d[rows, :], tokv[:, 0:20])], Btokv, D_ml)
            if ti == FT - 1:
                S.dma("sp", [(m_out[0:1, :], tokv[127:128, 20:24])], Btokv, D_out)
            if ti == TS:
                S.dma("sp", [(m_out[1 + i:2 + i, :], tokv[32 * i + 31:32 * i + 32, 20:24]) for i in range(4)], Btokv, D_out)

        getE = make_prefetcher(tile_order, load_x)
        run_interleaved([tileE(ti) for ti in tile_order], lag=5)

    if "E" in PH:
        run_phase(phase_E)

    def phase_F():
        gh_rep, Bgh = sb("gh_rep", [128, 1024], F32)
        S.dma("sp", [(gh_rep[:], ml_g_h[0, :].partition_broadcast(128))], D_in, Bgh)
        onesb, Bonesb = sb("onesb", [128, 1], BF16)
        pool(lambda e: e.memset(onesb[:], 1.0), [], [Bonesb])
        dd, Bdd = sb("dd", [4, NCH * 4], F32)
        decrep, Bdecrep = sb("decrep", [128, NCH * 4], F32)
        dve(lambda e: e.tensor_tensor(out=dd[:].rearrange("p (c h) -> p c h", h=4),
                                      in0=ident[0:4, 0:4].unsqueeze(1).to_broadcast([4, NCH, 4]),
                                      in1=dec_row[:, 0:NCH].unsqueeze(2).to_broadcast([4, NCH, 4]), op=ALU.mult),
            [Bident, Bdec], [Bdd])
        pdc, Bpdc = psrr.next()
        pe(lambda e: e.matmul(pdc[:, 0:NCH * 4], lhsT=ones[0:4, :], rhs=dd[:], start=True, stop=True), [Bones, Bdd], [Bpdc])
        act(lambda e: e.activation(out=decrep[:], in_=pdc[:, 0:NCH * 4], func=AF.Copy), [Bpdc], [Bdecrep])

        C, BC = sb("Cst", [128, 1024], F32)
        nst, Bnst = sb("nst", [128, 4], F32)
        Cd, BCd = sb("Cd", [128, 1024], F32)
        Cdb, BCdb = sb("Cdb", [128, 1024], BF16)
        nd, Bnd = sb("nd", [128, 4], F32)
        ndb, Bndb = sb("ndb", [128, 4], BF16)
        qc_rr = sbs("qc", [64, 512], F32, 3)
        kc_rr = sbs("kc", [64, 512], F32, 3)
        vc_rr = sbs("vc", [64, 1024], BF16, 3)
        oc_rr = sbs("oc", [64, 1024], F32, 4)
        tk_rr = sbs("tk", [64, 20], F32, 4)
        kwf_rr = sbs("kwf", [64, 512], F32, 2)
        kwb_rr = sbs("kwb", [64, 512], BF16, 2)
        QT_rr = sbs("QTm", [128, 256], BF16, 2)
        KWT_rr = sbs("KWTm", [128, 256], BF16, 2)
        Sm_rr = sbs("Sm", [64, 256], BF16, 2)
        d_rr = sbs("dsm", [64, 8], F32, 3)
        hh_rr = sbs("hh", [64, 1024], F32, 2)
        hgb_rr = sbs("hgb", [64, 1024], BF16, 2)
        cio, Bcio = sb("cio", [128, 1024], F32)
        nio, Bnio = sb("nio", [4, 128], F32)
        Nsets = [[PS[0], PS[1]], [PS[2], PS[3]]]
        U = [PS[4], PS[5]]
        m_rr = RR(PS[6:8])

        def init_zero():
            pool(lambda e: e.memset(C[:], 0.0), [], [BC])
            pool(lambda e: e.memset(nst[:], 0.0), [], [Bnst])

        def init_from(i):
            S.dma("sp", [(cio[:].rearrange("p (h c k) -> p h c k", h=4, c=2),
                          mC0[i].rearrange("h (c p) k -> p h c k", p=128))], D_in, Bcio)
            for g in range(2):
                pt, Bpt = m_rr.next()
                for j in range(4):
                    blk = g * 4 + j
                    pe(lambda e, j=j, blk=blk, pt=pt: e.transpose(out=pt[:, j * 128:(j + 1) * 128], in_=cio[:, blk * 128:(blk + 1) * 128],
                                                                  identity=ident[:]), [Bcio, Bident], [Bpt])
                act(lambda e, g=g, pt=pt: e.activation(out=C[:, g * 512:(g + 1) * 512], in_=pt[:, 0:512], func=AF.Copy), [Bpt], [BC])
            S.dma("sp", [(nio[:], mn0[i, :, :])], D_in, Bnio)
            pn, Bpn = m_rr.next()
            pe(lambda e: e.transpose(out=pn[:, 0:4], in_=nio[0:4, :], identity=ident[0:4, 0:4]), [Bnio, Bident], [Bpn])
            act(lambda e: e.activation(out=nst[:], in_=pn[:, 0:4], func=AF.Copy), [Bpn], [Bnst])

        def write_state(idx):
            for g in range(2):
                pt, Bpt = m_rr.next()
                for j in range(4):
                    blk = g * 4 + j
                    pe(lambda e, j=j, blk=blk, pt=pt: e.transpose(out=pt[:, j * 128:(j + 1) * 128], in_=C[:, blk * 128:(blk + 1) * 128],
                                                                  identity=ident[:]), [BC, Bident], [Bpt])
                act(lambda e, g=g, pt=pt: e.activation(out=cio[:, g * 512:(g + 1) * 512], in_=pt[:, 0:512], func=AF.Copy), [Bpt], [Bcio])
            S.dma("sp", [(C_out[idx].rearrange("h (c p) k -> p h c k", p=128),
                          cio[:].rearrange("p (h c k) -> p h c k", h=4, c=2))], Bcio, D_out)
            pn, Bpn = m_rr.next()
            pe(lambda e: e.transpose(out=pn[0:4, 0:128], in_=nst[:, 0:4], identity=ident[:]), [Bnst, Bident], [Bpn])
            act(lambda e: e.activation(out=nio[:], in_=pn[0:4, 0:128], func=AF.Copy), [Bpn], [Bnio])
            S.dma("sp", [(n_out[idx, :, :], nio[:])], Bnio, D_out)

        def chunkF(r0, L, c):
            st = {}
            N = Nsets[c % 2]

            def stage0():
                qc, Bqc = qc_rr.next(); kc, Bkc = kc_rr.next(); vc, Bvc = vc_rr.next(); oc, Boc = oc_rr.next(); tk, Btk = tk_rr.next()
                S.dma("sp", [(qc[0:L, :], mq_d[r0:r0 + L, :])], D_ml, Bqc)
                S.dma("sp", [(kc[0:L, :], mk_d[r0:r0 + L, :])], D_ml, Bkc)
                S.dma("sp", [(vc[0:L, :], mv_d[r0:r0 + L, :])], D_ml, Bvc)
                S.dma("sp", [(oc[0:L, :], mo_d[r0:r0 + L, :])], D_ml, Boc)
                S.dma("sp", [(tk[0:L, :], tok_d[r0:r0 + L, :])], D_ml, Btk)
                st.update(locals())

            def stage1():
                qc, Bqc, kc, Bkc, vc, Bvc, oc, Boc, tk, Btk = [st[k] for k in ('qc','Bqc','kc','Bkc','vc','Bvc','oc','Boc','tk','Btk')]
                kwf, Bkwf = kwf_rr.next(); kwb, Bkwb = kwb_rr.next()
                dve(lambda e: e.tensor_tensor(out=kwf[0:L, :].rearrange("p (h d) -> p h d", h=4),
                                              in0=kc[0:L, :].rearrange("p (h d) -> p h d", h=4),
                                              in1=tk[0:L, 0:4].unsqueeze(2).to_broadcast([L, 4, 128]), op=ALU.mult), [Bkc, Btk], [Bkwf])
                act(lambda e: e.activation(out=kwb[0:L, :], in_=kwf[0:L, :], func=AF.Copy), [Bkwf], [Bkwb])
                QT, BQT = QT_rr.next(); KWT, BKWT = KWT_rr.next()
                for (src, Bsrc, dst, Bdst) in ((qc, Bqc, QT, BQT), (kwf, Bkwf, KWT, BKWT)):
                    pt, Bpt = m_rr.next()
                    for h in range(4):
                        pe(lambda e, h=h, src=src, pt=pt: e.transpose(out=pt[:, h * L:(h + 1) * L], in_=src[0:L, h * 128:(h + 1) * 128],
                                                                      identity=ident[0:L, 0:L]), [Bsrc, Bident], [Bpt])
                    act(lambda e, dst=dst, pt=pt: e.activation(out=dst[:, 0:4 * L], in_=pt[:, 0:4 * L], func=AF.Copy), [Bpt], [Bdst])
                pS, BpS = m_rr.next()
                for h in range(4):
                    pe(lambda e, h=h: e.matmul(pS[0:L, h * L:(h + 1) * L], lhsT=KWT[:, h * L:(h + 1) * L], rhs=QT[:, h * L:(h + 1) * L],
                                               start=True, stop=True), [BKWT, BQT], [BpS])
                Sm, BSm = Sm_rr.next()
                dve(lambda e: e.tensor_tensor(out=Sm[0:L, 0:4 * L].rearrange("p (h t) -> p h t", h=4),
                                              in0=pS[0:L, 0:4 * L].rearrange("p (h t) -> p h t", h=4),
                                              in1=tri[0:L, 0:L].unsqueeze(1).to_broadcast([L, 4, L]), op=ALU.mult), [BpS, Btri], [BSm])
                st.update(locals())

            def stage2():
                qc, Bqc, kc, Bkc, vc, Bvc, oc, Boc, tk, Btk = [st[k] for k in ('qc','Bqc','kc','Bkc','vc','Bvc','oc','Boc','tk','Btk')]
                kwb, Bkwb, QT, BQT, Sm, BSm = [st[k] for k in ('kwb','Bkwb','QT','BQT','Sm','BSm')]
                dve(lambda e: e.tensor_tensor(out=Cd[:].rearrange("p (h v) -> p h v", h=4), in0=C[:].rearrange("p (h v) -> p h v", h=4),
                                              in1=decrep[:, c * 4:(c + 1) * 4].unsqueeze(2).to_broadcast([128, 4, 256]), op=ALU.mult),
                    [BC, Bdecrep], [BCd])
                act(lambda e: e.activation(out=Cdb[:], in_=Cd[:], func=AF.Copy), [BCd], [BCdb])
                dve(lambda e: e.tensor_tensor(out=nd[:], in0=nst[:], in1=decrep[:, c * 4:(c + 1) * 4], op=ALU.mult), [Bnst, Bdecrep], [Bnd])
                act(lambda e: e.activation(out=ndb[:], in_=nd[:], func=AF.Copy), [Bnd], [Bndb])
                pD, BpD = m_rr.next()
                for h in range(4):
                    n_ps, Bn = N[h // 2]
                    co = (h % 2) * 256
                    pe(lambda e, h=h, n_ps=n_ps, co=co: e.matmul(n_ps[0:L, co:co + 256], lhsT=QT[:, h * L:(h + 1) * L],
                                                                 rhs=Cdb[:, h * 256:(h + 1) * 256], start=True, stop=False), [BQT, BCdb], [Bn])
                    pe(lambda e, h=h, n_ps=n_ps, co=co: e.matmul(n_ps[0:L, co:co + 256], lhsT=Sm[0:L, h * L:(h + 1) * L],
                                                                 rhs=vc[0:L, h * 256:(h + 1) * 256], start=False, stop=True), [BSm, Bvc], [Bn])
                    pe(lambda e, h=h: e.matmul(pD[0:L, h:h + 1], lhsT=QT[:, h * L:(h + 1) * L], rhs=ndb[:, h:h + 1], start=True, stop=False),
                       [BQT, Bndb], [BpD])
                    pe(lambda e, h=h: e.matmul(pD[0:L, h:h + 1], lhsT=Sm[0:L, h * L:(h + 1) * L], rhs=onesb[0:L, 0:1], start=False, stop=True),
                       [BSm, Bonesb], [BpD])
                pD2, BpD2 = m_rr.next()
                for h in range(4):
                    u_ps, Bu = U[h // 2]
                    co = (h % 2) * 256
                    pe(lambda e, h=h, u_ps=u_ps, co=co: e.matmul(u_ps[:, co:co + 256], lhsT=kwb[0:L, h * 128:(h + 1) * 128],
                                                                 rhs=vc[0:L, h * 256:(h + 1) * 256], start=True, stop=True), [Bkwb, Bvc], [Bu])
                    pe(lambda e, h=h: e.matmul(pD2[:, h:h + 1], lhsT=kwb[0:L, h * 128:(h + 1) * 128], rhs=onesb[0:L, 0:1], start=True, stop=True),
                       [Bkwb, Bonesb], [BpD2])
                for g in range(2):
                    u_ps, Bu = U[g]
                    dve(lambda e, g=g, u_ps=u_ps: e.tensor_tensor(out=C[:, g * 512:(g + 1) * 512], in0=u_ps[:, 0:512], in1=Cd[:, g * 512:(g + 1) * 512],
                                                                  op=ALU.add), [Bu, BCd], [BC])
                dve(lambda e: e.tensor_tensor(out=nst[:], in0=pD2[:, 0:4], in1=nd[:], op=ALU.add), [BpD2, Bnd], [Bnst])
                ds, Bds = d_rr.next()
                dve(lambda e: e.tensor_tensor(out=ds[0:L, 0:4], in0=pD[0:L, 0:4], in1=tk[0:L, 4:8], op=ALU.mult), [BpD, Btk], [Bds])
                st["ds"] = (ds, Bds)

            def stage2b():
                oc, Boc, tk, Btk = [st[k] for k in ('oc', 'Boc', 'tk', 'Btk')]
                ds, Bds = st["ds"]
                act(lambda e: e.activation(out=ds[0:L, 0:4], in_=ds[0:L, 0:4], func=AF.Abs), [Bds], [Bds])
                dve(lambda e: e.tensor_tensor(out=ds[0:L, 0:4], in0=ds[0:L, 0:4], in1=tk[0:L, 16:20], op=ALU.max), [Bds, Btk], [Bds])
                dve(lambda e: e.reciprocal(out=ds[0:L, 0:4], in_=ds[0:L, 0:4]), [Bds], [Bds])
                dve(lambda e: e.tensor_tensor(out=ds[0:L, 0:4], in0=ds[0:L, 0:4], in1=tk[0:L, 4:8], op=ALU.mult), [Bds, Btk], [Bds])
                hh, Bhh = hh_rr.next()
                for h in range(4):
                    n_ps, Bn = N[h // 2]
                    co = (h % 2) * 256
                    act(lambda e, h=h, n_ps=n_ps, co=co: e.activation(out=hh[0:L, h * 256:(h + 1) * 256], in_=n_ps[0:L, co:co + 256], func=AF.Copy,
                                                                      scale=ds[0:L, h:h + 1]), [Bn, Bds], [Bhh])
                for h in range(4):
                    act(lambda e, h=h: e.activation(out=junk[0:L, 0:256], in_=hh[0:L, h * 256:(h + 1) * 256], func=AF.Square,
                                                    accum_out=ds[0:L, 4 + h:5 + h]), [Bhh], [Bjunk, Bds])
                rstd_from_ss(ds[0:L, 4:8], Bds, 256.0)
                dve(lambda e: e.tensor_tensor(out=hh[0:L, :].rearrange("p (h v) -> p h v", h=4), in0=hh[0:L, :].rearrange("p (h v) -> p h v", h=4),
                                              in1=ds[0:L, 4:8].unsqueeze(2).to_broadcast([L, 4, 256]), op=ALU.mult), [Bhh, Bds], [Bhh])
                pool(lambda e: e.tensor_tensor(out=hh[0:L, :], in0=hh[0:L, :], in1=gh_rep[0:L, :], op=ALU.mult), [Bhh, Bgh], [Bhh])
                hgb, Bhgb = hgb_rr.next()
                pool(lambda e: e.tensor_tensor(out=hgb[0:L, :], in0=hh[0:L, :], in1=oc[0:L, :], op=ALU.mult), [Bhh, Boc], [Bhgb])
                S.dma("sp", [(hg_d[r0:r0 + L, :], hgb[0:L, :])], Bhgb, D_hg)

            return stage0, stage1, stage2, stage2b

        sched = []

        def pre0():
            init_zero()
            pool(lambda e: e.memset(Cdb[:], 0.0), [], [BCdb])
            S.dma("sp", [(hg_d[TM * 128 + 64:(TM + 1) * 128, :], Cdb[0:64, :])], BCdb, D_hg)
        sched.append(chunkF(TM * 128, 64, 0) + (pre0, None))
        for t in range(FT):
            for j in range(2):
                last = (t == FT - 1 and j == 1)
                sched.append(chunkF(t * 128 + 64 * j, 64, 1 + 2 * t + j) + (None, (lambda: write_state(0)) if last else None))
        for i in range(4):
            sched.append(chunkF(TS * 128 + 32 * i, 32, 1 + 2 * FT + i) + ((lambda i=i: init_from(i)), (lambda i=i: write_state(1 + i))))
        n = len(sched)
        for k in range(n + 3):
            if k < n:
                sched[k][0]()
            if 1 <= k <= n:
                sched[k - 1][1]()
            if 2 <= k <= n + 1:
                s0_, s1_, s2_, s3_, pre_, post_ = sched[k - 2]
                if pre_ is not None:
                    pre_()
                s2_()
                if post_ is not None:
                    post_()
            if k >= 3:
                sched[k - 3][3]()

    def phase_G():
        load_w_down_half(w_down[1], 0)
        hgl_rr = sbs("hgl", [128, 1024], BF16, 3)

        def tileG(ti):
            hgl, Bhgl, x, Bx = getG(ti)
            yield
            hT, BhT = hT_rr.next()
            transpose_bf(hgl, Bhgl, hT[:], BhT)
            yield
            for half in range(2):
                p_, Bp_ = mm_group(hT, BhT, half * 512, 512)
                dve(lambda e, half=half, p_=p_: e.tensor_tensor(out=x[:, half * 512:(half + 1) * 512], in0=p_[:, 0:512],
                                                               in1=x[:, half * 512:(half + 1) * 512], op=ALU.add), [Bp_, Bx], [Bx])
                yield
            store_x(ti, x, Bx)
        def loadG(ti):
            hgl, Bhgl = hgl_rr.next()
            S.dma("sp", [(hgl[:], hg_d[ti * 128:(ti + 1) * 128, :])], D_hg, Bhgl)
            x, Bx = load_x(ti)
            return hgl, Bhgl, x, Bx
        getG = make_prefetcher(range(NT), loadG)
        run_interleaved([tileG(ti) for ti in range(NT)], lag=2)

    if "G" in PH:
        load_w(ml_w_out, 1024)
    if "F" in PH:
        run_phase(phase_F)
    if "G" in PH:
        run_phase(phase_G)
    if "H" in PH:
        run_phase(phase_FFN, 1, 0, False)
        run_phase(phase_FFN, 1, 1, True)

    if DEBUG:
        dbg = dout("dbg_x", [R, 1024])
        dbt_rr = sbs("dbt", [128, 1024], F32, 2)
        for ti in range(NT):
            t_, B_ = dbt_rr.next()
            S.dma("sp", [(t_[:], xs_d[ti * 128:(ti + 1) * 128, :])], D_xs[ti], B_)
            S.dma("sp", [(dbg[ti * 128:(ti + 1) * 128, :], t_[:])], B_, D_out)
    if DEBUG and "D" in PH and os.environ.get("DBH") == "1":
        dbh = dout("dbg_h", [len(groups), 128, 4096], BF16)
        dh_rr = sbs("dbh", [128, 4096], BF16, 2)
        for gi in range(len(groups)):
            t_, B_ = dh_rr.next()
            S.dma("sp", [(t_[:], hTs_d[gi, :, :])], D_hTs, B_)
            S.dma("sp", [(dbh[gi, :, :], t_[:])], B_, D_out)
    stats = S.emit()
    es.close()
    return nc, stats


_CACHE = {}


def kernel(x_prompt, x_sample, cache_fox_k, cache_fox_v, cache_fox_logf,
           state_mlstm_C, state_mlstm_n, state_mlstm_m, meta_tokens,
           g_mix, g_ffn, fox_w_in, fox_b_f, fox_g_q, fox_g_k, fox_w_out,
           mlstm_w_in, mlstm_b_i, mlstm_b_f, mlstm_g_h, mlstm_w_out,
           ffn_w_up, ffn_w_down, g_final):
    f = lambda a: np.ascontiguousarray(np.asarray(a, dtype=np.float32))
    x_prompt, x_sample = f(x_prompt), f(x_sample)
    NB, SEQ = x_prompt.shape[0], x_prompt.shape[1]
    FT = SEQ // 128
    PAST = cache_fox_k.shape[2]
    PB = PAST // 128
    TS, TM = FT, FT + 1
    key = (FT, PB)
    if key not in _CACHE:
        _CACHE[key] = build_program(FT, PB)
    nc, stats = _CACHE[key]
    in_maps = []
    zpad = np.zeros((112, 1024), np.float32)
    shared = {
        "g_mix": f(g_mix), "g_ffn": f(g_ffn), "g_final": f(g_final).reshape(1, 1024),
        "fox_w_in": f(fox_w_in)[0], "fox_b_f": f(fox_b_f), "fox_g_q": f(fox_g_q), "fox_g_k": f(fox_g_k),
        "fox_w_out": f(fox_w_out)[0], "ml_w_in": f(mlstm_w_in)[0], "ml_b_i": f(mlstm_b_i), "ml_b_f": f(mlstm_b_f),
        "ml_g_h": f(mlstm_g_h).reshape(1, 1024), "ml_w_out": f(mlstm_w_out)[0],
        "w_up": f(ffn_w_up), "w_down": f(ffn_w_down),
    }
    ckk, cvv, cll = f(cache_fox_k)[0], f(cache_fox_v)[0], f(cache_fox_logf)[0]
    sC, sn, sm = f(state_mlstm_C)[0], f(state_mlstm_n)[0], f(state_mlstm_m)[0]
    meta = f(meta_tokens)
    for c in range(NB):
        xin = np.concatenate([x_prompt[c], x_sample[4 * c:4 * c + 4].reshape(128, 1024), meta, zpad], axis=0)
        m = dict(shared)
        m["xin"] = xin
        m["ck"] = ckk[4 * c:4 * c + 4].reshape(4, PAST, 1024)
        m["cv"] = cvv[4 * c:4 * c + 4].reshape(4, PAST, 1024)
        m["cl"] = cll[4 * c:4 * c + 4]
        m["mC0"] = sC[4 * c:4 * c + 4]; m["mn0"] = sn[4 * c:4 * c + 4]; m["mm0"] = sm[4 * c:4 * c + 4]
        in_maps.append(m)
    res = run_bass_kernel_spmd(nc, in_maps, core_ids=list(range(NB)))
    rs = res.results
    _CACHE["last"] = rs

    def rows(name, w):
        a = np.stack([r[name] for r in rs])
        prompt = np.concatenate([a[:, TM * 128:TM * 128 + 16], a[:, 0:SEQ]], axis=1)
        samp = a[:, TS * 128:(TS + 1) * 128].reshape(4 * NB, 32, w)
        return prompt, samp
    yp, ys = rows("y_out", 1024)
    kp, ks = rows("k_out", 1024)
    vp, vs = rows("v_out", 1024)
    lp, ls = rows("lf_out", 16)
    Co = np.stack([r["C_out"] for r in rs]); no = np.stack([r["n_out"] for r in rs]); mo = np.stack([r["m_out"] for r in rs])
    L = SEQ + 16
    return (np.ascontiguousarray(yp[:, 16:]), ys,
            kp.reshape(1, NB, L, 16, 64), vp.reshape(1, NB, L, 16, 64), lp.reshape(1, NB, L, 16),
            np.ascontiguousarray(Co[:, 0])[None], np.ascontiguousarray(no[:, 0])[None], np.ascontiguousarray(mo[:, 0])[None],
            ks.reshape(1, 4 * NB, 32, 16, 64), vs.reshape(1, 4 * NB, 32, 16, 64), ls.reshape(1, 4 * NB, 32, 16),
            Co[:, 1:].reshape(1, 4 * NB, 4, 256, 128), no[:, 1:].reshape(1, 4 * NB, 4, 128), mo[:, 1:].reshape(1, 4 * NB, 4))
```
